# Optimizing a Trainium2 kernel written in Bass

```python
import math
import jax, jax.numpy as jnp
from jax import lax
import numpy as np

D_MODEL = 1024
BATCH = 4
SEQ = 8192
DEPTH = 2

D_MIX = D_MODEL
GLA_HEADS = 4
GLA_DV = D_MIX // 2 // GLA_HEADS
GLA_DK = GLA_DV // 2
GLA_LOWRANK = 16
GLA_TAU = 16.0
GLA_CHUNK = 64
SWA_HEADS = 8
SWA_KV_HEADS = 2
SWA_HD = (D_MIX - GLA_HEADS * GLA_DV) // SWA_HEADS
SWA_WINDOW = 128
D_FF = 2816
CONV_K = 3
EPS = 1e-6

GLA_Q = GLA_HEADS * GLA_DK
GLA_K = GLA_HEADS * GLA_DK
GLA_V = GLA_HEADS * GLA_DV
GLA_R = GLA_HEADS * GLA_DV
SWA_Q = SWA_HEADS * SWA_HD
SWA_K = SWA_KV_HEADS * SWA_HD
SWA_V = SWA_KV_HEADS * SWA_HD
SPLITS = (GLA_Q, GLA_K, GLA_V, GLA_R, GLA_LOWRANK, SWA_Q, SWA_K, SWA_V)
P_IN = sum(SPLITS)

kernel_name = "hymba_gla_swa_sink_convffn"


def alibi_slopes(n_heads):
    return np.array([2.0 ** (-8.0 * (h + 1) / n_heads) for h in range(n_heads)], dtype=np.float32)


def rms_norm(x, g):
    xf = x.astype(jnp.float32)
    y = xf * lax.rsqrt(jnp.mean(xf * xf, axis=-1, keepdims=True) + EPS)
    return (y * g.astype(jnp.float32)).astype(x.dtype)


def split_cols(z, sizes):
    idx = np.cumsum(sizes)[:-1].tolist()
    return jnp.split(z, idx, axis=-1)


def gla_chunked(q, k, v, log_a):
    B, L, H, dk = q.shape
    dv = v.shape[-1]
    C = GLA_CHUNK
    n = L // C

    def chunk(t):
        return t.astype(jnp.float32).reshape(B, n, C, H, t.shape[-1]).transpose(0, 3, 1, 2, 4)

    q, k, v, log_a = chunk(q), chunk(k), chunk(v), chunk(log_a)
    b = jnp.cumsum(log_a, axis=-2)
    b_last = b[..., -1:, :]
    qe = q * jnp.exp(b)
    ke = k * jnp.exp(-b)
    causal = jnp.tril(jnp.ones((C, C), dtype=bool))
    A = jnp.einsum('bhncd,bhnsd->bhncs', qe, ke)
    A = jnp.where(causal, A, 0.0)
    o_intra = jnp.einsum('bhncs,bhnsv->bhncv', A, v)
    kd = k * jnp.exp(b_last - b)
    chunk_state = jnp.einsum('bhncd,bhncv->bhndv', kd, v)
    decay = jnp.exp(b_last[..., 0, :])

    def step(S, inp):
        dec, cs = inp
        return S * dec[..., None] + cs, S

    S0 = jnp.zeros((B, H, dk, dv), jnp.float32)
    _, S_prev = lax.scan(step, S0, (jnp.moveaxis(decay, 2, 0), jnp.moveaxis(chunk_state, 2, 0)))
    S_prev = jnp.moveaxis(S_prev, 0, 2)
    o_inter = jnp.einsum('bhncd,bhndv->bhncv', qe, S_prev)
    o = o_intra + o_inter
    return o.transpose(0, 2, 3, 1, 4).reshape(B, L, H, dv)


def swa_sink_attention(q, k, v, sinks):
    B, L, Hq, hd = q.shape
    Hkv = k.shape[2]
    G = Hq // Hkv
    W = SWA_WINDOW
    nb = L // W
    qb = q.reshape(B, nb, W, Hkv, G, hd)
    kb = k.reshape(B, nb, W, Hkv, hd)
    vb = v.reshape(B, nb, W, Hkv, hd)
    pad = ((0, 0), (1, 0), (0, 0), (0, 0), (0, 0))
    kk = jnp.concatenate([jnp.pad(kb, pad)[:, :-1], kb], axis=2)
    vv = jnp.concatenate([jnp.pad(vb, pad)[:, :-1], vb], axis=2)
    s = jnp.einsum('bnqkgd,bnskd->bnkgqs', qb, kk,
                   preferred_element_type=jnp.float32) * (1.0 / math.sqrt(hd))
    i = jnp.arange(W)[:, None]
    j = jnp.arange(2 * W)[None, :]
    dist = (W + i - j).astype(jnp.float32)
    key_abs = (jnp.arange(nb)[:, None, None] - 1) * W + j[None]
    mask = (dist >= 0) & (dist < W) & (key_abs >= 0)
    slopes = jnp.asarray(alibi_slopes(Hq)).reshape(Hkv, G)
    s = s - slopes[:, :, None, None] * dist
    s = jnp.where(mask[None, :, None, None], s, -jnp.inf)
    sink = sinks.astype(jnp.float32).reshape(Hkv, G)[None, None, :, :, None, None]
    m = jnp.maximum(jnp.max(s, axis=-1, keepdims=True), sink)
    p = jnp.exp(s - m)
    denom = jnp.sum(p, axis=-1, keepdims=True) + jnp.exp(sink - m)
    p = p / denom
    o = jnp.einsum('bnkgqs,bnskd->bnqkgd', p, vv.astype(jnp.float32))
    return o.reshape(B, L, Hq * hd)


def causal_dwconv(u, w, b):
    K = w.shape[0]
    L = u.shape[1]
    up = jnp.pad(u, ((0, 0), (K - 1, 0), (0, 0)))
    out = up[:, 0:L] * w[0]
    for j in range(1, K):
        out = out + up[:, j:j + L] * w[j]
    return out + b


def hybrid_layer(x, mix_norm, w_in, w_alpha2, b_alpha, gla_norm, q_norm, k_norm, sinks,
                 w_out, ffn_norm, w_up, conv_w, conv_b, w_down):
    B, L, _ = x.shape
    dt = x.dtype
    h = rms_norm(x, mix_norm)
    z = h @ w_in
    gq, gk, gv, gr, glr, sq, sk, sv = split_cols(z, SPLITS)

    log_a = jax.nn.log_sigmoid((glr @ w_alpha2 + b_alpha).astype(jnp.float32)) / GLA_TAU
    gq = gq.reshape(B, L, GLA_HEADS, GLA_DK) * (GLA_DK ** -0.5)
    gk = gk.reshape(B, L, GLA_HEADS, GLA_DK)
    gv = gv.reshape(B, L, GLA_HEADS, GLA_DV)
    o_gla = gla_chunked(gq, gk, gv, log_a.reshape(B, L, GLA_HEADS, GLA_DK))
    o_gla = rms_norm(o_gla, gla_norm).reshape(B, L, GLA_V)
    o_gla = (o_gla * jax.nn.silu(gr.astype(jnp.float32))).astype(dt)

    sq = rms_norm(sq.reshape(B, L, SWA_HEADS, SWA_HD), q_norm)
    sk = rms_norm(sk.reshape(B, L, SWA_KV_HEADS, SWA_HD), k_norm)
    sv = sv.reshape(B, L, SWA_KV_HEADS, SWA_HD)
    o_swa = swa_sink_attention(sq, sk, sv, sinks).astype(dt)

    x = x + jnp.concatenate([o_gla, o_swa], axis=-1) @ w_out

    h2 = rms_norm(x, ffn_norm)
    u = causal_dwconv(h2 @ w_up, conv_w, conv_b)
    a, bval = jnp.split(u, 2, axis=-1)
    return x + (jax.nn.silu(a) * bval) @ w_down


def setup_inputs(seed: int = 0) -> dict:
    key = jax.random.key(seed)
    ks = jax.random.split(key, 16)
    f = jnp.float32

    def nrm(k, shape, scale):
        return jax.random.normal(k, shape, f) * scale

    return {
        "x": nrm(ks[0], (BATCH, SEQ, D_MODEL), 1.0),
        "mix_norm": 1.0 + nrm(ks[1], (DEPTH, D_MODEL), 0.02),
        "w_in": nrm(ks[2], (DEPTH, D_MODEL, P_IN), D_MODEL ** -0.5),
        "w_alpha2": nrm(ks[3], (DEPTH, GLA_LOWRANK, GLA_Q), GLA_LOWRANK ** -0.5),
        "b_alpha": nrm(ks[4], (DEPTH, GLA_Q), 0.1),
        "gla_norm": 1.0 + nrm(ks[5], (DEPTH, GLA_DV), 0.02),
        "q_norm": 1.0 + nrm(ks[6], (DEPTH, SWA_HD), 0.02),
        "k_norm": 1.0 + nrm(ks[7], (DEPTH, SWA_HD), 0.02),
        "sinks": nrm(ks[8], (DEPTH, SWA_HEADS), 0.5),
        "w_out": nrm(ks[9], (DEPTH, D_MIX, D_MODEL), (2.0 * D_MIX) ** -0.5),
        "ffn_norm": 1.0 + nrm(ks[10], (DEPTH, D_MODEL), 0.02),
        "w_up": nrm(ks[11], (DEPTH, D_MODEL, 2 * D_FF), D_MODEL ** -0.5),
        "conv_w": nrm(ks[12], (DEPTH, CONV_K, 2 * D_FF), CONV_K ** -0.5),
        "conv_b": nrm(ks[13], (DEPTH, 2 * D_FF), 0.02),
        "w_down": nrm(ks[14], (DEPTH, D_FF, D_MODEL), (2.0 * D_FF) ** -0.5),
    }


def reference(x, mix_norm, w_in, w_alpha2, b_alpha, gla_norm, q_norm, k_norm, sinks,
              w_out, ffn_norm, w_up, conv_w, conv_b, w_down):
    for l in range(DEPTH):
        x = hybrid_layer(x, mix_norm[l], w_in[l], w_alpha2[l], b_alpha[l], gla_norm[l],
                         q_norm[l], k_norm[l], sinks[l], w_out[l], ffn_norm[l], w_up[l],
                         conv_w[l], conv_b[l], w_down[l])
    return x
```

```python
import contextlib
import math
import numpy as np
import concourse.bass as bass
import concourse.mybir as mybir
from concourse.bass_utils import run_bass_kernel_spmd

F32 = mybir.dt.float32
BF16 = mybir.dt.bfloat16
AF = mybir.ActivationFunctionType
ALU = mybir.AluOpType

D = 1024
KT = 8
T = 512
NB = T // 128
NCH = T // 64
DFF = 2816
FT = 22
NL_FULL = 2
SEQ = 8192
NSLOT = 4
SLOT = 4096
EPS = 1e-6
NPRM = 256
DEBUG = set()

ENGS = ("pe", "act", "dve", "pool", "sp")


class Res:
    __slots__ = ("name", "w", "r")

    def __init__(self, name):
        self.name = name
        self.w = None
        self.r = {}


class Op:
    __slots__ = ("eng", "fn", "deps", "sig", "sem", "val", "is_dma", "inc", "phase")

    def __init__(self, eng, fn, is_dma, sem):
        self.eng = eng
        self.fn = fn
        self.deps = []
        self.sig = False
        self.sem = sem
        self.val = None
        self.is_dma = is_dma


class Sched:
    def __init__(self, nc):
        self.nc = nc
        self.ops = {e: [] for e in ENGS}
        self.dma_sem_keys = []
        self.dma_counts = {}
        self._res = {}
        self.phase = "setup"
        self.annotate = False

    def R(self, *key):
        r = self._res.get(key)
        if r is None:
            r = Res(key)
            self._res[key] = r
        return r

    def _add(self, eng, fn, reads, writes, is_dma=False, sem=None, extra=(), inc=16):
        op = Op(eng, fn, is_dma, sem)
        op.inc = inc
        op.phase = self.phase
        deps = list(extra)
        ps_reads = [r for r in reads if r.name[0] == "ps" and r not in writes]
        if ps_reads and eng != "pe":
            writes = list(writes) + ps_reads
        for r in reads:
            if r.w is not None:
                deps.append(r.w)
        for w in writes:
            if w.w is not None:
                deps.append(w.w)
            deps.extend(w.r.values())
        seen = set()
        for d in deps:
            if d is op or id(d) in seen:
                continue
            seen.add(id(d))
            if d.eng == eng and eng == "pe" and not d.is_dma and not is_dma:
                continue
            op.deps.append(d)
            d.sig = True
        rk = ("dma", id(op)) if is_dma else eng
        for r in reads:
            r.r[rk] = op
        for w in writes:
            w.w = op
            w.r = {}
        if is_dma:
            op.sig = True
            if sem not in self.dma_counts:
                self.dma_counts[sem] = 0
                self.dma_sem_keys.append(sem)
            self.dma_counts[sem] += 1
            op.val = inc * self.dma_counts[sem]
        self.ops[eng].append(op)
        return op

    def op(self, eng, fn, reads=(), writes=()):
        return self._add(eng, fn, reads, writes)

    def dma(self, eng, fn, reads=(), writes=(), sem=None, extra=(), inc=16):
        return self._add(eng, fn, reads, writes, is_dma=True, sem=sem, extra=extra, inc=inc)

    def emit(self, final_ops=()):
        nc = self.nc
        for e in ENGS:
            c = 0
            for op in self.ops[e]:
                if op.is_dma:
                    continue
                if op.sig:
                    c += 1
                    op.val = c
        with contextlib.ExitStack() as st:
            esem = {e: st.enter_context(nc.semaphore("c_" + e)) for e in ENGS}
            dsem = {k: st.enter_context(nc.semaphore("d_" + "_".join(str(x) for x in k)))
                    for k in self.dma_sem_keys}
            block = st.enter_context(nc.Block())

            def semof(op):
                return dsem[op.sem] if op.is_dma else esem[op.eng]

            def run(e, eng):
                waited = {}
                for op in self.ops[e]:
                    for d in op.deps:
                        s = semof(d)
                        k = id(s)
                        if waited.get(k, 0) >= d.val:
                            continue
                        waited[k] = d.val
                        eng.wait_ge(s, d.val)
                    ins = op.fn(eng)
                    if self.annotate:
                        ins.annotate(op.phase)
                    if op.is_dma:
                        ins.then_inc(dsem[op.sem], op.inc)
                    elif op.sig:
                        ins.then_inc(esem[e], 1)
                if e == "sp":
                    for d in final_ops:
                        eng.wait_ge(semof(d), d.val)

            @block.tensor
            def _(eng):
                run("pe", eng)

            @block.scalar
            def _(eng):
                run("act", eng)

            @block.vector
            def _(eng):
                run("dve", eng)

            @block.gpsimd
            def _(eng):
                run("pool", eng)

            @block.sync
            def _(eng):
                run("sp", eng)


WIN_CHUNKS = [
    ("FM0", 16, [(1536, 16)]),
    ("FM3", 512, [(1552, 512)]),
    ("FM4", 256, [(2064, 64), (2064, 64), (2128, 64), (2128, 64)]),
    ("TM0a", 256, [(2192, 64), (2192, 64), (2256, 64), (2256, 64)]),
    ("TM1", 512, [(512, 512)]),
    ("FM1", 512, [(0, 256), (256, 256)]),
    ("TM0b", 256, [(256, 256)]),
    ("FM2", 512, [(1024, 512)]),
]


def chunk_plan():
    plan = []
    for name, nc_, segs in WIN_CHUNKS:
        plan.append((name, "w_in", KT, nc_, segs))
    for oc in range(2):
        plan.append(("O%d" % oc, "w_out", KT, 512, [(512 * oc, 512)]))
    for ci in range(11):
        plan.append(("U%d" % ci, "w_up", KT, 512, [(256 * ci, 256), (DFF + 256 * ci, 256)]))
    for c in range(8):
        plan.append(("D%d" % c, "w_down", FT, 128, [(128 * c, 128)]))
    return plan


def build(n_tiles, n_layers, pipe=False):
    nc = bass.Bass("TRN2", target_bir_lowering=False)
    L = n_tiles * T
    NL = n_layers

    def dram(name, shape, dt, kind):
        return nc.dram_tensor(name, shape, dt, kind=kind).ap()

    x_d = dram("x", [n_tiles, 128, KT * T], F32, "ExternalInput")
    y_d = dram("y", [n_tiles, 128, KT * T], F32, "ExternalOutput")
    w_d = {
        "w_in": dram("w_in", [NL, D, 2320], F32, "ExternalInput"),
        "w_out": dram("w_out", [NL, D, D], F32, "ExternalInput"),
        "w_up": dram("w_up", [NL, D, 2 * DFF], F32, "ExternalInput"),
        "w_down": dram("w_down", [NL, DFF, D], F32, "ExternalInput"),
    }
    prm_d = dram("prm", [NL, 128, NPRM], F32, "ExternalInput")
    snk_d = dram("snk", [NL, 128, 1024], F32, "ExternalInput")
    wa_d = dram("wa", [NL, 32, 256], F32, "ExternalInput")
    cst_d = dram("cst", [128, 384 + 3072], F32, "ExternalInput")
    if pipe:
        obuf_d = dram("obuf", [128, KT * T], F32, "Internal")
        gbuf_d = [dram("gbuf%d" % i, [256, KT * T], F32, "Internal") for i in range(2)]

    plan = chunk_plan()
    scr = {}
    for l in range(NL):
        for (name, src, nk, ncols, segs) in plan:
            scr[(l, name)] = dram("scr_%d_%s" % (l, name), [128, nk * ncols], BF16, "Internal")

    st = contextlib.ExitStack()
    with st:
        def sb(name, shape, dt):
            return st.enter_context(nc.sbuf_tensor(name, shape, dt))

        S = Sched(nc)
        R = S.R

        cf32 = sb("cf32", [128, 384], F32)
        cbf = sb("cbf", [128, 3072], BF16)
        cvals = sb("cvals", [128, 4], F32)
        prm = [sb("prm%d" % l, [128, NPRM], F32) for l in range(NL)]
        esink = [sb("esink%d" % l, [128, 1024], F32) for l in range(NL)]
        wa = [sb("wa%d" % l, [32, 256], BF16) for l in range(NL)]
        xT = sb("xT", [128, KT, T], F32)
        hT = sb("hT", [128, KT, T + 2], BF16)
        sqs = [sb("sqs%d" % i, [128, T], BF16) for i in range(3)]
        rs = [sb("rs%d" % i, [128, T], F32) for i in range(3)]
        ring = [sb("ring%d" % i, [128, SLOT], BF16) for i in range(NSLOT)]
        glrT = sb("glrT", [32, T], BF16)
        la = sb("la", [128, NB, 256], F32)
        eb = sb("eb", [128, 2, T], F32)
        enb = sb("enb", [128, 2, T], F32)
        Eg = sb("Eg", [128, NB, 256], F32)
        qeT = sb("qeT", [128, 2, T], BF16)
        keT = sb("keT", [128, 2, T], BF16)
        sr = sb("sr", [128, 4, T], BF16)
        kd = sb("kd", [128, NB, 256], BF16)
        vtok = sb("vtok", [128, NB, 512], BF16)
        AT = sb("AT", [128, 2, 2, 512], BF16)
        Sf = [sb("Sf%d" % l, [128, 2, 2, 128], F32) for l in range(NL)]
        Sb = sb("Sb", [128, NCH, 2, 128], BF16)
        o_sb = [sb("o_sb%d" % i, [128, T], F32) for i in range(2)]
        oT = sb("oT", [128, KT, T], BF16)
        sqT = sb("sqT", [128, 4, T], BF16)
        skT = sb("skT", [128, 2, T], BF16)
        qn = sb("qn", [128, 4, T], BF16)
        kn = sb("kn", [128, 2, T], BF16)
        knp = [sb("knp%d" % l, [128, 2, 128], BF16) for l in range(NL)]
        v2 = sb("v2", [128, NB, 256], BF16)
        v2p = [sb("v2p%d" % l, [128, 256], BF16) for l in range(NL)]
        pex = [sb("pex%d" % i, [128, 2, 512], BF16) for i in range(2)]
        PT = [sb("PT%d" % i, [128, 2, 512], BF16) for i in range(2)]
        rden = [sb("rden%d" % i, [128, 512], F32) for i in range(2)]
        t1 = [sb("t1_%d" % i, [128, 2, 256], F32) for i in range(3)]
        ub = [sb("u%d" % i, [128, 2, 256], F32) for i in range(6)]
        gT = sb("gT", [128, FT, T], BF16)
        h2halo = [sb("h2halo%d" % l, [128, KT, 2], BF16) for l in range(NL)]
        gst = [sb("gst%d" % i, [128, T], F32) for i in range(4)]
        PS = st.enter_context(nc.psum_tensor("PS", [128, 8, 512], F32))

        ident = cf32[:, 0:128]
        tri_incl = cf32[:, 128:256]
        tri_excl = cf32[:, 256:384]
        ones_d = cbf[:, 0:128]
        ones_bd = cbf[:, 128:256]
        ones_dv = cbf[:, 256:384]
        ones1 = cbf[:, 384:512]
        maskA = cbf[:, 512:1024]

        def EBt(k, j):
            o = 1024 + (2 * k + j) * 512
            return cbf[:, o:o + 512]

        c_eps = cvals[:, 0:1]
        c_ln8 = cvals[:, 1:2]
        c_one = cvals[:, 2:3]

        def MM(out, lhsT, rhs, start, stop, reads, writes):
            return S.op("pe", lambda e: e.matmul(out, lhsT=lhsT, rhs=rhs, start=start, stop=stop), reads, writes)

        def TR(out, in_, reads, writes):
            return S.op("pe", lambda e: e.transpose(out, in_, ident), list(reads) + [R("cf32")], writes)

        def ACT(out, in_, func, reads, writes, bias=None, scale=None):
            kw = {}
            if bias is not None:
                kw["bias"] = bias
            if scale is not None:
                kw["scale"] = scale
            return S.op("act", lambda e: e.activation(out=out, in_=in_, func=func, **kw), reads, writes)

        def TT(eng, out, in0, in1, op, reads, writes):
            return S.op(eng, lambda e: e.tensor_tensor(out=out, in0=in0, in1=in1, op=op), reads, writes)

        def STT(eng, out, in0, scalar, in1, op0, op1, reads, writes):
            return S.op(eng, lambda e: e.scalar_tensor_tensor(out=out, in0=in0, scalar=scalar, in1=in1,
                                                              op0=op0, op1=op1), reads, writes)

        def CP(eng, out, in_, reads, writes):
            if eng == "act":
                return ACT(out, in_, AF.Copy, reads, writes)
            return S.op(eng, lambda e: e.tensor_copy(out=out, in_=in_), reads, writes)

        def MEMSET(eng, ap, val, writes):
            return S.op(eng, lambda e: e.memset(ap, val), (), writes)

        bank_ctr = [0]

        def bank():
            while True:
                b = bank_ctr[0] % 8
                bank_ctr[0] += 1
                if b not in reserved:
                    return b

        pair_ctr = [0]

        def bank_pair():
            while True:
                p = pair_ctr[0] % 4
                pair_ctr[0] += 1
                if 2 * p not in reserved and 2 * p + 1 not in reserved:
                    return p

        def RB(b):
            return R("ps", b)

        S.dma("sp", lambda e: e.dma_start(out=cf32[:], in_=cst_d[:, 0:384]), writes=[R("cf32")], sem=("cf32",))
        S.dma("pool", lambda e: e.dma_start(out=cbf[:], in_=cst_d[:, 384:384 + 3072]), writes=[R("cbf")], sem=("cbf",))
        MEMSET("dve", cvals[:, 0:1], EPS, [R("cvals")])
        MEMSET("dve", cvals[:, 1:2], math.log(0.125), [R("cvals")])
        MEMSET("dve", cvals[:, 2:3], 1.0, [R("cvals")])
        MEMSET("dve", glrT[:], 1.0, [R("glrT")])
        for l in range(NL):
            S.dma("sp", lambda e, l=l: e.dma_start(out=prm[l][:], in_=prm_d[l]), writes=[R("prm", l)], sem=("prm", l))
            S.dma("sp", lambda e, l=l: e.dma_start(out=esink[l][:], in_=snk_d[l]), writes=[R("esink", l)], sem=("snk", l))
            S.dma("pool", lambda e, l=l: e.dma_start(out=wa[l][:], in_=wa_d[l]), writes=[R("wa", l)], sem=("wa", l))
            ACT(esink[l][:], esink[l][:], AF.Exp, [R("esink", l)], [R("esink", l)])
            MEMSET("dve", Sf[l][:], 0.0, [R("Sfq", l, bf, i, hh) for bf in range(2) for i in range(2) for hh in range(2)])
            MEMSET("dve", knp[l][:], 0.0, [R("knp", l)])
            MEMSET("dve", v2p[l][:], 0.0, [R("v2p", l)])
            MEMSET("dve", h2halo[l][:], 0.0, [R("h2halo", l)])

        conv_last = {}
        for l in range(NL):
            for (name, src, nk, ncols, segs) in plan:
                if "noconv" in DEBUG and name != "FM0":
                    continue
                dst = scr[(l, name)].rearrange("p (k c) -> p k c", k=nk)
                srcv = w_d[src][l].rearrange("(k p) c -> p k c", p=128)
                off = 0
                for (s0, n) in segs:
                    op = S.dma("pool", lambda e, dst=dst, srcv=srcv, off=off, s0=s0, n=n:
                               e.dma_start(out=dst[:, :, off:off + n], in_=srcv[:, :, s0:s0 + n]),
                               sem=("cv", l, name))
                    off += n
                    conv_last[(l, name)] = op

        seq = []
        for t in range(n_tiles):
            for l in range(NL):
                for (name, src, nk, ncols, segs) in plan:
                    seq.append((l, name, nk, ncols))
        ring_state = {"loaded": 0, "next": 0}

        def ensure_loaded(upto):
            while ring_state["loaded"] <= min(upto, len(seq) - 1):
                n = ring_state["loaded"]
                l, name, nk, ncols = seq[n]
                s = n % NSLOT
                S.dma("sp", lambda e, s=s, l=l, name=name, nk=nk, ncols=ncols:
                      e.dma_start(out=ring[s][:, 0:nk * ncols], in_=scr[(l, name)]),
                      writes=[R("ring", s)], sem=("ring", s), extra=[conv_last[(l, name)]])
                ring_state["loaded"] += 1

        def next_chunk(expect_l, expect_name):
            n = ring_state["next"]
            l, name, nk, ncols = seq[n]
            assert (l, name) == (expect_l, expect_name), (l, name, expect_l, expect_name)
            ensure_loaded(n + NSLOT - 1)
            ring_state["next"] += 1
            s = n % NSLOT
            return ring[s][:, 0:nk * ncols].rearrange("p (k c) -> p k c", k=nk), R("ring", s)

        reserved = set()
        nstate = {"b": None, "n": 0}

        def norm_sq(kt):
            if nstate["b"] is None:
                nstate["b"] = bank()
                nstate["n"] = 0
                reserved.add(nstate["b"])
            b = nstate["b"]
            i = nstate["n"]
            q = sqs[i % 3]
            ACT(q[:], xT[:, kt, :], AF.Square, [R("xT", kt)], [R("sqs", i % 3)])
            MM(PS[:, b, :], ones_d, q[:], i == 0, i == KT - 1, [R("cbf"), R("sqs", i % 3)], [RB(b)])
            nstate["n"] += 1

        def rmsnorm(l, gcol):
            assert nstate["n"] == KT
            b = nstate["b"]
            nstate["b"] = None
            reserved.discard(b)
            r = rs[0]
            ACT(r[:], PS[:, b, :], AF.Ln, [RB(b), R("cvals")], [R("rs", 0)], bias=c_eps)
            ACT(r[:], r[:], AF.Exp, [R("rs", 0)], [R("rs", 0)], scale=-0.5)
            for kt in range(KT):
                STT("dve", hT[:, kt, 2:2 + T], xT[:, kt, :], prm[l][:, gcol + kt:gcol + kt + 1], r[:],
                    ALU.mult, ALU.mult, [R("xT", kt), R("prm", l), R("rs", 0)], [R("hT", kt)])

        gpre = {"t": -1, "n": 0}

        def g_load(t, kt):
            gv = gbuf_d[t % 2][0:128, :].rearrange("p (k q) -> p k q", k=KT)
            i = kt % 4
            S.dma("sp", lambda e: e.dma_start(out=gst[i][:], in_=gv[:, kt, :]),
                  reads=[R("gbuf", t % 2)], writes=[R("gst", i)], sem=("gst", i))

        def prefetch_g(t):
            if not pipe or t >= n_tiles:
                return
            gpre["t"], gpre["n"] = t, 4
            for kt in range(4):
                g_load(t, kt)

        def load_piece(t, kt):
            xv = x_d[t].rearrange("p (k q) -> p k q", k=KT)
            S.dma("sp", lambda e: e.dma_start(out=xT[:, kt, :], in_=xv[:, kt, :]),
                  writes=[R("xT", kt)], sem=("xin", kt))
            if pipe:
                if not (gpre["t"] == t and kt < gpre["n"]):
                    g_load(t, kt)
                i = kt % 4
                STT("dve", xT[:, kt, :], gst[i][:], prm[0][:, 20:21], xT[:, kt, :], ALU.mult, ALU.add,
                    [R("gst", i), R("prm", 0), R("xT", kt)], [R("xT", kt)])

        def load_x(t):
            for kt in range(KT):
                load_piece(t, kt)
                norm_sq(kt)

        def send_x(t, outs):
            ob = [R("obuf", c) for c in range(KT)]
            if t + 2 < n_tiles:
                S.dma("pool", lambda e: e.collective_compute("AllGather", ALU.bypass,
                                                             replica_groups=[[0, 1], [2, 3], [4, 5], [6, 7]],
                                                             ins=[obuf_d], outs=[gbuf_d[t % 2]]),
                      reads=ob, writes=[R("gbuf", t % 2)], sem=("cc",), inc=1)
            o = S.dma("sp", lambda e: e.dma_start(out=y_d[t], in_=obuf_d), reads=ob, sem=("yout",))
            outs.append(o)

        def store_y(t, outs):
            o = S.dma("sp", lambda e: e.dma_start(out=y_d[t], in_=xT[:].rearrange("p k q -> p (k q)")),
                      reads=[R("xT", kt) for kt in range(KT)], sem=("yout",))
            outs.append(o)

        def hTr(kt):
            return hT[:, kt, 2:2 + T]

        def mixer(t, l):
            P = prm[l]
            chunks = {}

            def getw(name):
                if name not in chunks:
                    chunks[name] = next_chunk(l, name)
                return chunks[name]

            def g_fm0():
                w, wr = getw("FM0")
                b = bank()
                for kt in range(KT):
                    MM(PS[0:16, b, :], w[:, kt, 0:16], hTr(kt), kt == 0, kt == KT - 1, [wr, R("hT", kt)], [RB(b)])
                CP("act", glrT[0:16, :], PS[0:16, b, :], [RB(b)], [R("glrT")])

            def g_logits():
                for g in range(2):
                    b = bank()
                    for h2 in range(2):
                        nb = 2 * g + h2
                        MM(PS[:, b, h2 * 256:(h2 + 1) * 256], glrT[0:32, nb * 128:(nb + 1) * 128], wa[l][0:32, :],
                           True, True, [R("glrT"), R("wa", l)], [RB(b)])
                    ACT(la[:, 2 * g:2 * g + 2, :], PS[:, b, :].rearrange("p (a q) -> p a q", a=2), AF.Exp,
                        [RB(b)], [R("la", g)], scale=-1.0)
                ACT(la[:], la[:], AF.Ln, [R("la", 0), R("la", 1), R("cvals")], [R("la", 0), R("la", 1)], bias=c_one)

            def g_cumfm():
                for i in range(2):
                    b = bank()
                    for nb in range(NB):
                        MM(PS[:, b, nb * 128:(nb + 1) * 128], la[:, nb, i * 128:(i + 1) * 128], tri_incl,
                           True, True, [R("la", nb // 2), R("cf32")], [RB(b)])
                    ACT(eb[:, i, :], PS[:, b, :], AF.Exp, [RB(b)], [R("eb", i)])
                    ACT(enb[:, i, :], PS[:, b, :], AF.Exp, [RB(b)], [R("enb", i)], scale=-1.0)

            def g_cumtm():
                for g in range(2):
                    b = bank()
                    for h2 in range(2):
                        nb = 2 * g + h2
                        MM(PS[:, b, h2 * 256:(h2 + 1) * 256], tri_excl, la[:, nb, :], True, True,
                           [R("la", g), R("cf32")], [RB(b)])
                    ACT(Eg[:, 2 * g:2 * g + 2, :], PS[:, b, :].rearrange("p (a q) -> p a q", a=2), AF.Exp,
                        [RB(b)], [R("Eg", g)])

            qk_dq = []

            def qknorm(b, raw, rawres, dst, dstres, gcol, is_q, idx):
                q = sqs[idx % 3]
                ACT(q[:], PS[:, b, :], AF.Square, [RB(b)], [R("sqs", idx % 3)])
                CP("dve", raw, PS[:, b, :], [RB(b)], [rawres])
                b2 = bank()
                MM(PS[:, b2, :], ones_bd, q[:], True, True, [R("cbf"), R("sqs", idx % 3)], [RB(b2)])
                r = rs[1 + idx % 2]
                rr = R("rs", 1 + idx % 2)

                def tail():
                    ACT(r[:], PS[:, b2, :], AF.Ln, [RB(b2), R("cvals")], [rr], bias=c_eps)
                    if is_q:
                        ACT(r[:], r[:], AF.Exp, [rr, R("cvals")], [rr], scale=-0.5, bias=c_ln8)
                    else:
                        ACT(r[:], r[:], AF.Exp, [rr], [rr], scale=-0.5)
                    STT("dve", dst, raw, P[:, gcol:gcol + 1], r[:], ALU.mult, ALU.mult,
                        [rawres, R("prm", l), rr], [dstres])
                qk_dq.append(tail)
                while len(qk_dq) > 1:
                    qk_dq.pop(0)()

            def fm_block(name, c0):
                w, wr = getw(name)
                b = bank()
                for kt in range(KT):
                    MM(PS[:, b, :], w[:, kt, c0:c0 + 128], hTr(kt), kt == 0, kt == KT - 1,
                       [wr, R("hT", kt)], [RB(b)])
                return b

            def tm_block(name, nb, ncols):
                w, wr = getw(name)
                b = bank()
                for kt in range(KT):
                    MM(PS[:, b, 0:ncols], hT[:, kt, 2 + nb * 128:2 + (nb + 1) * 128], w[:, kt, :], kt == 0, kt == KT - 1,
                       [wr, R("hT", kt)], [RB(b)])
                return b

            def fm3_blk(i):
                b = fm_block("FM3", i * 128)
                qknorm(b, sqT[:, i, :], R("sqT", i), qn[:, i, :], R("qn", i), 16, True, i)

            def fm4_blk(k):
                b = fm_block("FM4", k * 128)
                qknorm(b, skT[:, k, :], R("skT", k), kn[:, k, :], R("kn", k), 17, False, 4 + k)

            def tm0a_blk(nb):
                b = tm_block("TM0a", nb, 256)
                CP("act", v2[:, nb, :], PS[:, b, 0:256], [RB(b)], [R("v2", nb)])

            def fm1_blk(blk):
                b = fm_block("FM1", blk * 128)
                i = blk % 2
                if blk < 2:
                    STT("dve", qeT[:, i, :], PS[:, b, :], 0.125, eb[:, i, :], ALU.mult, ALU.mult,
                        [RB(b), R("eb", i)], [R("qeT", i)])
                else:
                    TT("dve", keT[:, i, :], PS[:, b, :], enb[:, i, :], ALU.mult, [RB(b), R("enb", i)], [R("keT", i)])

            def fm2_blk(h):
                b = fm_block("FM2", h * 128)
                ACT(sr[:, h, :], PS[:, b, :], AF.Silu, [RB(b)], [R("sr", h)])

            def tm0b_blk(nb):
                b = tm_block("TM0b", nb, 256)
                TT("dve", kd[:, nb, :], PS[:, b, 0:256], Eg[:, nb, :], ALU.mult, [RB(b), R("Eg", nb // 2)], [R("kd", nb)])

            def tm1_blk(nb):
                b = tm_block("TM1", nb, 512)
                CP("act", vtok[:, nb, :], PS[:, b, :], [RB(b)], [R("vtok", nb)])

            S.phase = "mixA_%d" % t
            A = [lambda i=i: fm3_blk(i) for i in range(4)] + [lambda k=k: fm4_blk(k) for k in range(2)] \
                + [lambda nb=nb: tm0a_blk(nb) for nb in range(NB)] \
                + [lambda nb=nb: tm1_blk(nb) for nb in range(NB)]

            def popA(n):
                for _ in range(n):
                    if A:
                        A.pop(0)()

            g_fm0()
            popA(3)
            g_logits()
            popA(3)
            while qk_dq:
                qk_dq.pop(0)()
            g_cumfm()
            popA(4)
            g_cumtm()
            while qk_dq:
                qk_dq.pop(0)()
            popA(len(A))
            Bq = [lambda i=i: fm1_blk(i) for i in range(4)] + [lambda nb=nb: tm0b_blk(nb) for nb in range(NB)]
            Fq = [lambda h=h: fm2_blk(h) for h in range(4)]

            def popB(n):
                for _ in range(n):
                    if Bq:
                        Bq.pop(0)()

            S.phase = "swa_%d" % t
            def swa_stage1(it):
                nb, k = it // 2, it % 2
                jts = [0, 1]
                if t == 0 and nb == 0:
                    jts = [1]
                c0 = jts[0] * 256
                bs = [bank(), bank()]
                srcs = {}
                for jt in jts:
                    if jt == 0:
                        if nb == 0:
                            ksrc, kres = knp[l][:, k, :], R("knp", l)
                            vsrc, vres = v2p[l][:, k * 128:(k + 1) * 128], R("v2p", l)
                        else:
                            ksrc, kres = kn[:, k, (nb - 1) * 128:nb * 128], R("kn", k)
                            vsrc, vres = v2[:, nb - 1, k * 128:(k + 1) * 128], R("v2", nb - 1)
                    else:
                        ksrc, kres = kn[:, k, nb * 128:(nb + 1) * 128], R("kn", k)
                        vsrc, vres = v2[:, nb, k * 128:(k + 1) * 128], R("v2", nb)
                    srcs[jt] = (vsrc, vres)
                    for hh in range(2):
                        fp = hh * 64
                        MM(PS[:, bs[hh], jt * 256:(jt + 1) * 256].rearrange("p (a q) -> p a q", a=2),
                           ksrc[fp:fp + 64, :], qn[fp:fp + 64, 2 * k:2 * k + 2, nb * 128:(nb + 1) * 128],
                           True, True, [kres, R("qn", 2 * k), R("qn", 2 * k + 1)], [RB(bs[hh])])
                pe_, pr = pex[it % 2], R("pex", it % 2)
                pt_, ptr = PT[it % 2], R("PT", it % 2)
                for hh in range(2):
                    ACT(pe_[:, hh, c0:512], PS[:, bs[hh], c0:512], AF.Exp, [RB(bs[hh])], [pr])
                for hh in range(2):
                    TT("pool", pt_[:, hh, c0:512], pe_[:, hh, c0:512], EBt(k, hh)[:, c0:512], ALU.mult, [pr, R("cbf")], [ptr])
                if pipe and t == 2 and nb == 0:
                    S.op("dve", lambda e, pt_=pt_: e.tensor_scalar(out=pt_[:, :, 0:256], in0=pt_[:, :, 0:256],
                                                                   scalar1=P[:, 21:22], scalar2=None, op0=ALU.mult),
                         [ptr, R("prm", l)], [ptr])
                return (nb, k, jts, srcs, pt_, ptr)

            def swa_stage2(st):
                nb, k, jts, srcs, pt_, ptr = st
                bo = bank()
                bd = bank()
                for jt in jts:
                    vsrc, vres = srcs[jt]
                    MM(PS[:, bo, :].rearrange("p (a q) -> p a q", a=2), vsrc, pt_[:, :, jt * 256:(jt + 1) * 256],
                       jt == jts[0], jt == jts[-1], [vres, ptr], [RB(bo)])
                for jt in jts:
                    MM(PS[:, bd, :].rearrange("p (a q) -> p a q", a=2), ones1, pt_[:, :, jt * 256:(jt + 1) * 256],
                       jt == jts[0], jt == jts[-1], [R("cbf"), ptr], [RB(bd)])
                rd, rdr = rden[(2 * nb + k) % 2], R("rden", (2 * nb + k) % 2)
                TT("dve", rd[:], PS[:, bd, :], esink[l][:, k * 512:(k + 1) * 512], ALU.add,
                   [RB(bd), R("esink", l)], [rdr])
                ACT(rd[:], rd[:], AF.Ln, [rdr], [rdr])
                ACT(rd[:], rd[:], AF.Exp, [rdr], [rdr], scale=-1.0)
                for hh in range(2):
                    fp = hh * 64
                    TT("dve", oT[fp:fp + 64, 4 + 2 * k:4 + 2 * k + 2, nb * 128:(nb + 1) * 128],
                       PS[fp:fp + 64, bo, hh * 256:(hh + 1) * 256].rearrange("p (a q) -> p a q", a=2),
                       rd[fp:fp + 64, hh * 256:(hh + 1) * 256].rearrange("p (a q) -> p a q", a=2),
                       ALU.mult, [RB(bo), rdr], [R("oT", 4 + 2 * k), R("oT", 4 + 2 * k + 1)])

            def gla_sb0():
                if "noSb0" not in DEBUG:
                    CP("pool", Sb[:, 0, :, :], Sf[l][:, 0, :, :], [R("Sfq", l, 0, i, hh) for i in range(2) for hh in range(2)], [R("Sb", 0)])

            def gla_at(g):
                bb = [bank(), bank()]
                for nbl in range(2):
                    nb = 2 * g + nbl
                    for i in range(2):
                        for hh in range(2):
                            fp = hh * 64
                            co = (nbl * 2 + i) * 128
                            MM(PS[:, bb[hh], co:co + 128], keT[fp:fp + 64, i, nb * 128:(nb + 1) * 128],
                               qeT[fp:fp + 64, i, nb * 128:(nb + 1) * 128], True, True,
                               [R("keT", i), R("qeT", i)], [RB(bb[hh])])
                for hh in range(2):
                    TT("dve", AT[:, g, hh, :], PS[:, bb[hh], :], maskA, ALU.mult, [RB(bb[hh]), R("cbf")], [R("AT", g, hh)])

            def gla_scan(c):
                nb, hp = c // 2, (c % 2) * 64
                b = bank()
                for i in range(2):
                    MM(PS[:, b, i * 256:(i + 1) * 256], kd[hp:hp + 64, nb, i * 128:(i + 1) * 128],
                       vtok[hp:hp + 64, nb, i * 256:(i + 1) * 256], True, True,
                       [R("kd", nb), R("vtok", nb)], [RB(b)])
                for i in range(2):
                    for hh in range(2):
                        fp = hh * 64
                        col = 64 * c + 63
                        STT("dve", Sf[l][fp:fp + 64, (c + 1) % 2, i, :], Sf[l][fp:fp + 64, c % 2, i, :],
                            eb[fp:fp + 64, i, col:col + 1],
                            PS[fp:fp + 64, b, i * 256 + hh * 128:i * 256 + hh * 128 + 128], ALU.mult, ALU.add,
                            [R("Sfq", l, c % 2, i, hh), R("eb", i), RB(b)], [R("Sfq", l, (c + 1) % 2, i, hh)])
                if c + 1 < NCH:
                    CP("pool", Sb[:, c + 1, :, :], Sf[l][:, (c + 1) % 2, :, :],
                       [R("Sfq", l, (c + 1) % 2, i, hh) for i in range(2) for hh in range(2)], [R("Sb", c + 1)])

            Gq = [gla_sb0] + [lambda g=g: gla_at(g) for g in range(2)] + [lambda c=c: gla_scan(c) for c in range(NCH)]

            def popG(n):
                for _ in range(n):
                    if Gq:
                        Gq.pop(0)()

            pend = None
            for it in range(2 * NB):
                cur = swa_stage1(it)
                if Bq:
                    popB(1)
                else:
                    popG(1)
                if pend is not None:
                    swa_stage2(pend)
                if Bq:
                    popB(1)
                else:
                    popG(2)
                    if Fq:
                        Fq.pop(0)()
                pend = cur
            swa_stage2(pend)
            popB(len(Bq))
            popG(len(Gq))
            while Fq:
                Fq.pop(0)()
            CP("pool", knp[l][:], kn[:, :, T - 128:T], [R("kn", 0), R("kn", 1)], [R("knp", l)])
            CP("pool", v2p[l][:], v2[:, NB - 1, :], [R("v2", NB - 1)], [R("v2p", l)])

            if "stop6" in DEBUG:
                return
            S.phase = "gla_%d" % t
            gl_dq = []
            if "stop4b" in DEBUG:
                return
            for h in range(4):
                i, hh = h // 2, h % 2
                fp = hh * 64
                b = bank()
                for nb in range(NB):
                    aco = ((nb % 2) * 2 + i) * 128
                    MM(PS[:, b, nb * 128:(nb + 1) * 128], vtok[:, nb, h * 128:(h + 1) * 128],
                       AT[:, nb // 2, hh, aco:aco + 128], True, False, [R("vtok", nb), R("AT", nb // 2, hh)], [RB(b)])
                    for half in range(2):
                        c = 2 * nb + half
                        MM(PS[:, b, c * 64:(c + 1) * 64], Sb[fp:fp + 64, c, i, :], qeT[fp:fp + 64, i, c * 64:(c + 1) * 64],
                           False, half == 1, [R("Sb", c), R("qeT", i)], [RB(b)])
                osb = o_sb[h % 2]
                ro = R("o_sb", h % 2)
                q = sqs[h % 3]
                CP("act", osb[:], PS[:, b, :], [RB(b)], [ro])
                ACT(q[:], PS[:, b, :], AF.Square, [RB(b)], [R("sqs", h % 3)])
                r = rs[1 + h % 2]
                rr = R("rs", 1 + h % 2)

                def gtail(r=r, rr=rr, q=q, osb=osb, ro=ro, h=h):
                    b2 = bank()
                    MM(PS[:, b2, :], ones_dv, q[:], True, True, [R("cbf"), R("sqs", h % 3)], [RB(b2)])
                    ACT(r[:], PS[:, b2, :], AF.Ln, [RB(b2), R("cvals")], [rr], bias=c_eps)
                    ACT(r[:], r[:], AF.Exp, [rr], [rr], scale=-0.5)
                    STT("dve", osb[:], osb[:], P[:, 18:19], r[:], ALU.mult, ALU.mult, [ro, R("prm", l), rr], [ro])
                    TT("pool", oT[:, h, :], osb[:], sr[:, h, :], ALU.mult, [ro, R("sr", h)], [R("oT", h)])
                gl_dq.append(gtail)
                while len(gl_dq) > 1:
                    gl_dq.pop(0)()
            while gl_dq:
                gl_dq.pop(0)()

            if "stop5" in DEBUG:
                return
            S.phase = "outproj_%d" % t
            nq = []
            for oc in range(2):
                w, wr = next_chunk(l, "O%d" % oc)
                for cc in range(4):
                    c = 4 * oc + cc
                    b = bank()
                    for kt in range(KT):
                        MM(PS[:, b, :], w[:, kt, cc * 128:(cc + 1) * 128], oT[:, kt, :], kt == 0, kt == KT - 1,
                           [wr, R("oT", kt)], [RB(b)])
                    TT("dve", xT[:, c, :], PS[:, b, :], xT[:, c, :], ALU.add, [RB(b), R("xT", c)], [R("xT", c)])
                    nq.append(c)
                    while len(nq) > 2:
                        norm_sq(nq.pop(0))
            while nq:
                norm_sq(nq.pop(0))

        def ffn(t, l):
            P = prm[l]
            S.phase = "ffnnorm_%d" % t
            CP("pool", hT[:, :, 0:2], h2halo[l][:], [R("h2halo", l)], [R("hThalo")])
            rmsnorm(l, 8)
            S.phase = "ffnup_%d" % t
            CP("pool", h2halo[l][:], hT[:, :, T:T + 2], [R("hT", kt) for kt in range(KT)], [R("h2halo", l)])
            dq = []

            def flush(keep):
                while len(dq) > keep:
                    dq.pop(0)()

            for ci in range(11):
                w, wr = next_chunk(l, "U%d" % ci)
                for blk in range(4):
                    isb = blk >= 2
                    j = 2 * ci + blk % 2
                    fidx = j + (FT if isb else 0)
                    pc = 32 + fidx * 4
                    p = bank_pair()
                    rb = [RB(2 * p), RB(2 * p + 1)]
                    for kt in range(KT):
                        for half in range(2):
                            MM(PS[:, 2 * p + half, 0:258], w[:, kt, blk * 128:(blk + 1) * 128],
                               hT[:, kt, half * 256:half * 256 + 258], kt == 0, kt == KT - 1,
                               [wr, R("hT", kt), R("hThalo")], [rb[half]])
                    fbi = ci * 4 + blk
                    tt, ttr = t1[fbi % 3], R("t1", fbi % 3)
                    uidx = ((ci % 2) * 2 + blk) if not isb else (4 + blk - 2)
                    uu, uur = ub[uidx], R("u", uidx)
                    ACT(tt[:], PS[:, 2 * p:2 * p + 2, 2:258], AF.Identity, rb + [R("prm", l)], [ttr],
                        scale=P[:, pc + 2:pc + 3], bias=P[:, pc + 3:pc + 4])
                    STT("dve", tt[:], PS[:, 2 * p:2 * p + 2, 1:257], P[:, pc + 1:pc + 2], tt[:], ALU.mult, ALU.add,
                        rb + [R("prm", l), ttr], [ttr])
                    STT("dve", uu[:], PS[:, 2 * p:2 * p + 2, 0:256], P[:, pc:pc + 1], tt[:], ALU.mult, ALU.add,
                        rb + [R("prm", l), ttr], [uur])
                    if not isb:
                        dq.append(lambda uu=uu, uur=uur: ACT(uu[:], uu[:], AF.Silu, [uur], [uur]))
                    else:
                        aidx = (ci % 2) * 2 + blk - 2
                        ua, uar = ub[aidx], R("u", aidx)
                        dq.append(lambda ua=ua, uar=uar, uu=uu, uur=uur, j=j: TT(
                            "pool", gT[:, j, :].rearrange("p (a q) -> p a q", a=2), ua[:], uu[:], ALU.mult,
                            [uar, uur], [R("gT", j)]))
                    flush(2)
            flush(0)
            S.phase = "ffndown_%d" % t
            prefetch_g(t + 1)
            lq = []
            for c in range(8):
                w, wr = next_chunk(l, "D%d" % c)
                b = bank()
                for ft in range(FT):
                    MM(PS[:, b, :], w[:, ft, :], gT[:, ft, :], ft == 0, ft == FT - 1, [wr, R("gT", ft)], [RB(b)])
                TT("dve", xT[:, c, :], PS[:, b, :], xT[:, c, :], ALU.add, [RB(b), R("xT", c)], [R("xT", c)])
                if pipe:
                    S.dma("sp", lambda e, c=c: e.dma_start(out=obuf_d[:, c * T:(c + 1) * T], in_=xT[:, c, :]),
                          reads=[R("xT", c)], writes=[R("obuf", c)], sem=("obuf", c))
                    if t + 1 < n_tiles:
                        load_piece(t + 1, c)
                        lq.append(c)
                        while len(lq) > 2:
                            norm_sq(lq.pop(0))
            while lq:
                norm_sq(lq.pop(0))

        outs = []
        if pipe:
            for kt in range(KT):
                MEMSET("dve", xT[:, kt, :], 0.0, [R("xT", kt)])
            for i in range(2):
                S.dma("sp", lambda e, i=i: e.dma_start(out=gbuf_d[i][0:128, :], in_=xT[:].rearrange("p k q -> p (k q)")),
                      reads=[R("xT", kt) for kt in range(KT)], writes=[R("gbuf", i)], sem=("ginit", i))
        for t in range(n_tiles):
            S.phase = "loadx_%d" % t
            if t == 0 or not pipe:
                load_x(t)
            for l in range(NL):
                if "nolayer" in DEBUG:
                    continue
                S.phase = "mixnorm_%d" % t
                rmsnorm(l, 0)
                if "nomixer" not in DEBUG:
                    mixer(t, l)
                if "noffn" not in DEBUG:
                    ffn(t, l)
            S.phase = "send_%d" % t
            if pipe:
                send_x(t, outs)
            else:
                store_y(t, outs)
        S.annotate = "annotate" in DEBUG
        S.emit(final_ops=outs)
    return nc


def alibi_slopes(n):
    return np.array([2.0 ** (-8.0 * (h + 1) / n) for h in range(n)], dtype=np.float64)


def make_consts():
    c = np.zeros((128, 384 + 3072), np.float32)
    c[:, 0:128] = np.eye(128, dtype=np.float32)
    s = np.arange(128)[:, None]
    q = np.arange(128)[None, :]
    same = (s // 64) == (q // 64)
    c[:, 128:256] = np.where(same & (s <= q), -1.0 / 16.0, 0.0)
    c[:, 256:384] = np.where(same & (s > q), -1.0 / 16.0, 0.0)
    o = 384
    c[:, o + 0:o + 128] = 1.0 / 1024.0
    bd = (np.arange(128)[:, None] // 64) == (np.arange(128)[None, :] // 64)
    c[:, o + 128:o + 256] = np.where(bd, 1.0 / 64.0, 0.0)
    c[:, o + 256:o + 384] = 1.0 / 128.0
    c[:, o + 384:o + 512] = 1.0
    mA = np.where(same & (s <= q), 1.0, 0.0)
    c[:, o + 512:o + 1024] = np.tile(mA, (1, 4))
    sl = alibi_slopes(8)
    for k in range(2):
        for hh in range(2):
            tab = np.zeros((128, 512), np.float64)
            for jt in range(2):
                for ii in range(2):
                    h = 4 * k + 2 * ii + hh
                    if jt == 1:
                        dist = (q - s).astype(np.float64)
                        keep = q >= s
                    else:
                        dist = (128 + q - s).astype(np.float64)
                        keep = q < s
                    co = jt * 256 + ii * 128
                    tab[:, co:co + 128] = np.where(keep, np.exp(-sl[h] * dist), 0.0)
            oo = o + 1024 + (2 * k + hh) * 512
            c[:, oo:oo + 512] = tab
    return c


def pack_params(inp, l):
    p = np.zeros((128, NPRM), np.float32)
    p[:, 0:8] = inp["mix_norm"][l].reshape(8, 128).T
    p[:, 8:16] = inp["ffn_norm"][l].reshape(8, 128).T
    p[:, 16] = np.tile(inp["q_norm"][l], 2)
    p[:, 17] = np.tile(inp["k_norm"][l], 2)
    p[:, 18] = inp["gla_norm"][l]
    cw = inp["conv_w"][l].reshape(3, 44, 128)
    cb = inp["conv_b"][l].reshape(44, 128)
    blk = np.zeros((128, 44, 4), np.float32)
    blk[:, :, 0:3] = cw.transpose(2, 1, 0)
    blk[:, :, 3] = cb.T
    p[:, 32:32 + 176] = blk.reshape(128, 176)
    return p


def pack_sinks(inp, l):
    s = np.zeros((128, 1024), np.float32)
    for k in range(2):
        for hh in range(2):
            for ii in range(2):
                h = 4 * k + 2 * ii + hh
                slot = hh * 2 + ii
                s[:, k * 512 + slot * 128:k * 512 + (slot + 1) * 128] = inp["sinks"][l][h]
    return s


def pack_wa(inp, l):
    w = np.zeros((32, 256), np.float32)
    w[0:16] = inp["w_alpha2"][l]
    w[16] = inp["b_alpha"][l]
    return w


def to_fm(xs):
    n = xs.shape[0] // T
    return np.ascontiguousarray(xs.reshape(n, T, KT, 128).transpose(0, 3, 2, 1).reshape(n, 128, KT * T))


def from_fm(yf):
    n = yf.shape[0]
    return np.ascontiguousarray(yf.reshape(n, 128, KT, T).transpose(0, 3, 2, 1).reshape(n * T, D))


_NC_CACHE = {}


def get_nc(n_tiles, n_layers, pipe=False):
    key = (n_tiles, n_layers, pipe)
    if key not in _NC_CACHE:
        _NC_CACHE[key] = build(n_tiles, n_layers, pipe)
    return _NC_CACHE[key]


def host_maps(inp, seqs, n_layers, layer0=0, roles=None):
    ls = list(range(layer0, layer0 + n_layers))
    f = lambda a: np.ascontiguousarray(np.asarray(a, dtype=np.float32))

    def shared_for(ls):
        return {
            "w_in": f(np.asarray(inp["w_in"])[ls]),
            "w_out": f(np.asarray(inp["w_out"])[ls]),
            "w_up": f(np.asarray(inp["w_up"])[ls]),
            "w_down": f(np.asarray(inp["w_down"])[ls]),
            "prm": np.stack([pack_params(inp, l) for l in ls]),
            "snk": np.stack([pack_sinks(inp, l) for l in ls]),
            "wa": np.stack([pack_wa(inp, l) for l in ls]),
            "cst": make_consts(),
        }

    maps = []
    if roles is None:
        shared = shared_for(ls)
        for s in seqs:
            m = dict(shared)
            m["x"] = f(s)
            maps.append(m)
        return maps
    per_layer = {}
    for s, role in zip(seqs, roles):
        if role not in per_layer:
            per_layer[role] = shared_for([role])
        m = dict(per_layer[role])
        prm = m["prm"].copy()
        prm[:, :, 20] = float(role)
        prm[:, :, 21] = 1.0 - float(role)
        m["prm"] = prm
        m["x"] = f(s)
        maps.append(m)
    return maps


def kernel(**inputs):
    inp = {k: np.asarray(v) for k, v in inputs.items()}
    x = inp["x"]
    B, L, _ = x.shape
    n_tiles = L // T
    nc = get_nc(n_tiles + 2, 1, True)
    seqs, roles = [], []
    zt = np.zeros((2 * T, D), np.float32)
    for c in range(8):
        b, role = c // 2, c % 2
        if role == 0:
            seqs.append(to_fm(np.concatenate([x[b], zt], axis=0)))
        else:
            seqs.append(np.zeros((n_tiles + 2, 128, KT * T), np.float32))
        roles.append(role)
    maps = host_maps(inp, seqs, 1, roles=roles)
    res = run_bass_kernel_spmd(nc, maps, core_ids=list(range(8)))
    out = np.stack([from_fm(np.asarray(res.results[2 * b + 1]["y"]))[2 * T:] for b in range(B)], axis=0)
    return out.astype(np.float32)
```

```python
import contextlib
import math
import numpy as np
import concourse.bass as bass
import concourse.mybir as mybir
from concourse.bass_utils import run_bass_kernel_spmd

F32 = mybir.dt.float32
BF16 = mybir.dt.bfloat16
AF = mybir.ActivationFunctionType
ALU = mybir.AluOpType

D = 1024
KT = 8
T = 512
NB = T // 128
NCH = T // 64
DFF = 2816
FT = 22
NL_FULL = 2
SEQ = 8192
NSLOT = 4
SLOT = 4096
EPS = 1e-6
NPRM = 256
DEBUG = set()

ENGS = ("pe", "act", "dve", "pool", "sp")


class Res:
    __slots__ = ("name", "w", "r")

    def __init__(self, name):
        self.name = name
        self.w = None
        self.r = {}


class Op:
    __slots__ = ("eng", "fn", "deps", "sig", "sem", "val", "is_dma", "inc", "phase")

    def __init__(self, eng, fn, is_dma, sem):
        self.eng = eng
        self.fn = fn
        self.deps = []
        self.sig = False
        self.sem = sem
        self.val = None
        self.is_dma = is_dma


class Sched:
    def __init__(self, nc):
        self.nc = nc
        self.ops = {e: [] for e in ENGS}
        self.dma_sem_keys = []
        self.dma_counts = {}
        self._res = {}
        self.phase = "setup"
        self.annotate = False

    def R(self, *key):
        r = self._res.get(key)
        if r is None:
            r = Res(key)
            self._res[key] = r
        return r

    def _add(self, eng, fn, reads, writes, is_dma=False, sem=None, extra=(), inc=16):
        op = Op(eng, fn, is_dma, sem)
        op.inc = inc
        op.phase = self.phase
        deps = list(extra)
        ps_reads = [r for r in reads if r.name[0] == "ps" and r not in writes]
        if ps_reads and eng != "pe":
            writes = list(writes) + ps_reads
        for r in reads:
            if r.w is not None:
                deps.append(r.w)
        for w in writes:
            if w.w is not None:
                deps.append(w.w)
            deps.extend(w.r.values())
        seen = set()
        for d in deps:
            if d is op or id(d) in seen:
                continue
            seen.add(id(d))
            if d.eng == eng and eng == "pe" and not d.is_dma and not is_dma:
                continue
            op.deps.append(d)
            d.sig = True
        rk = ("dma", id(op)) if is_dma else eng
        for r in reads:
            r.r[rk] = op
        for w in writes:
            w.w = op
            w.r = {}
        if is_dma:
            op.sig = True
            if sem not in self.dma_counts:
                self.dma_counts[sem] = 0
                self.dma_sem_keys.append(sem)
            self.dma_counts[sem] += 1
            op.val = inc * self.dma_counts[sem]
        self.ops[eng].append(op)
        return op

    def op(self, eng, fn, reads=(), writes=()):
        return self._add(eng, fn, reads, writes)

    def dma(self, eng, fn, reads=(), writes=(), sem=None, extra=(), inc=16):
        return self._add(eng, fn, reads, writes, is_dma=True, sem=sem, extra=extra, inc=inc)

    def emit(self, final_ops=()):
        nc = self.nc
        for e in ENGS:
            c = 0
            for op in self.ops[e]:
                if op.is_dma:
                    continue
                if op.sig:
                    c += 1
                    op.val = c
        with contextlib.ExitStack() as st:
            esem = {e: st.enter_context(nc.semaphore("c_" + e)) for e in ENGS}
            dsem = {k: st.enter_context(nc.semaphore("d_" + "_".join(str(x) for x in k)))
                    for k in self.dma_sem_keys}
            block = st.enter_context(nc.Block())

            def semof(op):
                return dsem[op.sem] if op.is_dma else esem[op.eng]

            def run(e, eng):
                waited = {}
                for op in self.ops[e]:
                    for d in op.deps:
                        s = semof(d)
                        k = id(s)
                        if waited.get(k, 0) >= d.val:
                            continue
                        waited[k] = d.val
                        eng.wait_ge(s, d.val)
                    ins = op.fn(eng)
                    if self.annotate:
                        ins.annotate(op.phase)
                    if op.is_dma:
                        ins.then_inc(dsem[op.sem], op.inc)
                    elif op.sig:
                        ins.then_inc(esem[e], 1)
                if e == "sp":
                    for d in final_ops:
                        eng.wait_ge(semof(d), d.val)

            @block.tensor
            def _(eng):
                run("pe", eng)

            @block.scalar
            def _(eng):
                run("act", eng)

            @block.vector
            def _(eng):
                run("dve", eng)

            @block.gpsimd
            def _(eng):
                run("pool", eng)

            @block.sync
            def _(eng):
                run("sp", eng)


WIN_CHUNKS = [
    ("FM0", 16, [(1536, 16)]),
    ("FM3", 512, [(1552, 512)]),
    ("FM4", 256, [(2064, 64), (2064, 64), (2128, 64), (2128, 64)]),
    ("TM0a", 256, [(2192, 64), (2192, 64), (2256, 64), (2256, 64)]),
    ("TM1", 512, [(512, 512)]),
    ("FM1", 512, [(0, 256), (256, 256)]),
    ("TM0b", 256, [(256, 256)]),
    ("FM2", 512, [(1024, 512)]),
]


def chunk_plan():
    plan = []
    for name, nc_, segs in WIN_CHUNKS:
        plan.append((name, "w_in", KT, nc_, segs))
    for oc in range(2):
        plan.append(("O%d" % oc, "w_out", KT, 512, [(512 * oc, 512)]))
    for ci in range(11):
        plan.append(("U%d" % ci, "w_up", KT, 512, [(256 * ci, 256), (DFF + 256 * ci, 256)]))
    for c in range(8):
        plan.append(("D%d" % c, "w_down", FT, 128, [(128 * c, 128)]))
    return plan


def build(n_tiles, n_layers, pipe=False):
    nc = bass.Bass("TRN2", target_bir_lowering=False)
    L = n_tiles * T
    NL = n_layers

    def dram(name, shape, dt, kind):
        return nc.dram_tensor(name, shape, dt, kind=kind).ap()

    x_d = dram("x", [n_tiles, 128, KT * T], F32, "ExternalInput")
    y_d = dram("y", [n_tiles, 128, KT * T], F32, "ExternalOutput")
    w_d = {
        "w_in": dram("w_in", [NL, D, 2320], F32, "ExternalInput"),
        "w_out": dram("w_out", [NL, D, D], F32, "ExternalInput"),
        "w_up": dram("w_up", [NL, D, 2 * DFF], F32, "ExternalInput"),
        "w_down": dram("w_down", [NL, DFF, D], F32, "ExternalInput"),
    }
    prm_d = dram("prm", [NL, 128, NPRM], F32, "ExternalInput")
    snk_d = dram("snk", [NL, 128, 1024], F32, "ExternalInput")
    wa_d = dram("wa", [NL, 32, 256], F32, "ExternalInput")
    cst_d = dram("cst", [128, 384 + 3072], F32, "ExternalInput")
    if pipe:
        obuf_d = dram("obuf", [128, KT * T], F32, "Internal")
        gbuf_d = [dram("gbuf%d" % i, [256, KT * T], F32, "Internal") for i in range(2)]

    plan = chunk_plan()
    scr = {}
    for l in range(NL):
        for (name, src, nk, ncols, segs) in plan:
            scr[(l, name)] = dram("scr_%d_%s" % (l, name), [128, nk * ncols], BF16, "Internal")

    st = contextlib.ExitStack()
    with st:
        def sb(name, shape, dt):
            return st.enter_context(nc.sbuf_tensor(name, shape, dt))

        S = Sched(nc)
        R = S.R

        cf32 = sb("cf32", [128, 384], F32)
        cbf = sb("cbf", [128, 3072], BF16)
        cvals = sb("cvals", [128, 4], F32)
        prm = [sb("prm%d" % l, [128, NPRM], F32) for l in range(NL)]
        esink = [sb("esink%d" % l, [128, 1024], F32) for l in range(NL)]
        wa = [sb("wa%d" % l, [32, 256], BF16) for l in range(NL)]
        xT = sb("xT", [128, KT, T], F32)
        hT = sb("hT", [128, KT, T + 2], BF16)
        sqs = [sb("sqs%d" % i, [128, T], BF16) for i in range(3)]
        rs = [sb("rs%d" % i, [128, T], F32) for i in range(3)]
        ring = [sb("ring%d" % i, [128, SLOT], BF16) for i in range(NSLOT)]
        glrT = sb("glrT", [32, T], BF16)
        la = sb("la", [128, NB, 256], F32)
        eb = sb("eb", [128, 2, T], F32)
        enb = sb("enb", [128, 2, T], F32)
        Eg = sb("Eg", [128, NB, 256], F32)
        qeT = sb("qeT", [128, 2, T], BF16)
        keT = sb("keT", [128, 2, T], BF16)
        sr = sb("sr", [128, 4, T], BF16)
        kd = sb("kd", [128, NB, 256], BF16)
        vtok = sb("vtok", [128, NB, 512], BF16)
        AT = sb("AT", [128, 2, 2, 512], BF16)
        Sf = [sb("Sf%d" % l, [128, 2, 2, 128], F32) for l in range(NL)]
        Sb = sb("Sb", [128, NCH, 2, 128], BF16)
        o_sb = [sb("o_sb%d" % i, [128, T], F32) for i in range(2)]
        oT = sb("oT", [128, KT, T], BF16)
        sqT = sb("sqT", [128, 4, T], BF16)
        skT = sb("skT", [128, 2, T], BF16)
        qn = sb("qn", [128, 4, T], BF16)
        kn = sb("kn", [128, 2, T], BF16)
        knp = [sb("knp%d" % l, [128, 2, 128], BF16) for l in range(NL)]
        v2 = sb("v2", [128, NB, 256], BF16)
        v2p = [sb("v2p%d" % l, [128, 256], BF16) for l in range(NL)]
        pex = [sb("pex%d" % i, [128, 2, 512], BF16) for i in range(2)]
        PT = [sb("PT%d" % i, [128, 2, 512], BF16) for i in range(2)]
        rden = [sb("rden%d" % i, [128, 512], F32) for i in range(2)]
        t1 = [sb("t1_%d" % i, [128, 2, 256], F32) for i in range(3)]
        ub = [sb("u%d" % i, [128, 2, 256], F32) for i in range(6)]
        gT = sb("gT", [128, FT, T], BF16)
        h2halo = [sb("h2halo%d" % l, [128, KT, 2], BF16) for l in range(NL)]
        gst = [sb("gst%d" % i, [128, T], F32) for i in range(4)]
        PS = st.enter_context(nc.psum_tensor("PS", [128, 8, 512], F32))

        ident = cf32[:, 0:128]
        tri_incl = cf32[:, 128:256]
        tri_excl = cf32[:, 256:384]
        ones_d = cbf[:, 0:128]
        ones_bd = cbf[:, 128:256]
        ones_dv = cbf[:, 256:384]
        ones1 = cbf[:, 384:512]
        maskA = cbf[:, 512:1024]

        def EBt(k, j):
            o = 1024 + (2 * k + j) * 512
            return cbf[:, o:o + 512]

        c_eps = cvals[:, 0:1]
        c_ln8 = cvals[:, 1:2]
        c_one = cvals[:, 2:3]

        def MM(out, lhsT, rhs, start, stop, reads, writes):
            return S.op("pe", lambda e: e.matmul(out, lhsT=lhsT, rhs=rhs, start=start, stop=stop), reads, writes)

        def TR(out, in_, reads, writes):
            return S.op("pe", lambda e: e.transpose(out, in_, ident), list(reads) + [R("cf32")], writes)

        def ACT(out, in_, func, reads, writes, bias=None, scale=None):
            kw = {}
            if bias is not None:
                kw["bias"] = bias
            if scale is not None:
                kw["scale"] = scale
            return S.op("act", lambda e: e.activation(out=out, in_=in_, func=func, **kw), reads, writes)

        def TT(eng, out, in0, in1, op, reads, writes):
            return S.op(eng, lambda e: e.tensor_tensor(out=out, in0=in0, in1=in1, op=op), reads, writes)

        def STT(eng, out, in0, scalar, in1, op0, op1, reads, writes):
            return S.op(eng, lambda e: e.scalar_tensor_tensor(out=out, in0=in0, scalar=scalar, in1=in1,
                                                              op0=op0, op1=op1), reads, writes)

        def CP(eng, out, in_, reads, writes):
            if eng == "act":
                return ACT(out, in_, AF.Copy, reads, writes)
            return S.op(eng, lambda e: e.tensor_copy(out=out, in_=in_), reads, writes)

        def MEMSET(eng, ap, val, writes):
            return S.op(eng, lambda e: e.memset(ap, val), (), writes)

        bank_ctr = [0]

        def bank():
            while True:
                b = bank_ctr[0] % 8
                bank_ctr[0] += 1
                if b not in reserved:
                    return b

        pair_ctr = [0]

        def bank_pair():
            while True:
                p = pair_ctr[0] % 4
                pair_ctr[0] += 1
                if 2 * p not in reserved and 2 * p + 1 not in reserved:
                    return p

        def RB(b):
            return R("ps", b)

        S.dma("sp", lambda e: e.dma_start(out=cf32[:], in_=cst_d[:, 0:384]), writes=[R("cf32")], sem=("cf32",))
        S.dma("pool", lambda e: e.dma_start(out=cbf[:], in_=cst_d[:, 384:384 + 3072]), writes=[R("cbf")], sem=("cbf",))
        MEMSET("dve", cvals[:, 0:1], EPS, [R("cvals")])
        MEMSET("dve", cvals[:, 1:2], math.log(0.125), [R("cvals")])
        MEMSET("dve", cvals[:, 2:3], 1.0, [R("cvals")])
        MEMSET("dve", glrT[:], 1.0, [R("glrT")])
        for l in range(NL):
            S.dma("sp", lambda e, l=l: e.dma_start(out=prm[l][:], in_=prm_d[l]), writes=[R("prm", l)], sem=("prm", l))
            S.dma("sp", lambda e, l=l: e.dma_start(out=esink[l][:], in_=snk_d[l]), writes=[R("esink", l)], sem=("snk", l))
            S.dma("pool", lambda e, l=l: e.dma_start(out=wa[l][:], in_=wa_d[l]), writes=[R("wa", l)], sem=("wa", l))
            ACT(esink[l][:], esink[l][:], AF.Exp, [R("esink", l)], [R("esink", l)])
            MEMSET("dve", Sf[l][:], 0.0, [R("Sfq", l, bf, i, hh) for bf in range(2) for i in range(2) for hh in range(2)])
            MEMSET("dve", knp[l][:], 0.0, [R("knp", l)])
            MEMSET("dve", v2p[l][:], 0.0, [R("v2p", l)])
            MEMSET("dve", h2halo[l][:], 0.0, [R("h2halo", l)])

        conv_last = {}
        for l in range(NL):
            for (name, src, nk, ncols, segs) in plan:
                if "noconv" in DEBUG and name != "FM0":
                    continue
                dst = scr[(l, name)].rearrange("p (k c) -> p k c", k=nk)
                srcv = w_d[src][l].rearrange("(k p) c -> p k c", p=128)
                off = 0
                for (s0, n) in segs:
                    op = S.dma("pool", lambda e, dst=dst, srcv=srcv, off=off, s0=s0, n=n:
                               e.dma_start(out=dst[:, :, off:off + n], in_=srcv[:, :, s0:s0 + n]),
                               sem=("cv", l, name))
                    off += n
                    conv_last[(l, name)] = op

        seq = []
        for t in range(n_tiles):
            for l in range(NL):
                for (name, src, nk, ncols, segs) in plan:
                    seq.append((l, name, nk, ncols))
        ring_state = {"loaded": 0, "next": 0}

        def ensure_loaded(upto):
            while ring_state["loaded"] <= min(upto, len(seq) - 1):
                n = ring_state["loaded"]
                l, name, nk, ncols = seq[n]
                s = n % NSLOT
                S.dma("sp", lambda e, s=s, l=l, name=name, nk=nk, ncols=ncols:
                      e.dma_start(out=ring[s][:, 0:nk * ncols], in_=scr[(l, name)]),
                      writes=[R("ring", s)], sem=("ring", s), extra=[conv_last[(l, name)]])
                ring_state["loaded"] += 1

        def next_chunk(expect_l, expect_name):
            n = ring_state["next"]
            l, name, nk, ncols = seq[n]
            assert (l, name) == (expect_l, expect_name), (l, name, expect_l, expect_name)
            ensure_loaded(n + NSLOT - 1)
            ring_state["next"] += 1
            s = n % NSLOT
            return ring[s][:, 0:nk * ncols].rearrange("p (k c) -> p k c", k=nk), R("ring", s)

        reserved = set()
        nstate = {"b": None, "n": 0}

        def norm_sq(kt):
            if nstate["b"] is None:
                nstate["b"] = bank()
                nstate["n"] = 0
                reserved.add(nstate["b"])
            b = nstate["b"]
            i = nstate["n"]
            q = sqs[i % 3]
            ACT(q[:], xT[:, kt, :], AF.Square, [R("xT", kt)], [R("sqs", i % 3)])
            MM(PS[:, b, :], ones_d, q[:], i == 0, i == KT - 1, [R("cbf"), R("sqs", i % 3)], [RB(b)])
            nstate["n"] += 1

        def rmsnorm(l, gcol):
            assert nstate["n"] == KT
            b = nstate["b"]
            nstate["b"] = None
            reserved.discard(b)
            r = rs[0]
            ACT(r[:], PS[:, b, :], AF.Ln, [RB(b), R("cvals")], [R("rs", 0)], bias=c_eps)
            ACT(r[:], r[:], AF.Exp, [R("rs", 0)], [R("rs", 0)], scale=-0.5)
            for kt in range(KT):
                STT("dve", hT[:, kt, 2:2 + T], xT[:, kt, :], prm[l][:, gcol + kt:gcol + kt + 1], r[:],
                    ALU.mult, ALU.mult, [R("xT", kt), R("prm", l), R("rs", 0)], [R("hT", kt)])

        loaded = set()

        def stage_load(t, kt):
            if t >= n_tiles or (t, kt) in loaded:
                return
            loaded.add((t, kt))
            i = kt % 2
            xv = x_d[t].rearrange("p (k q) -> p k q", k=KT)
            gv = gbuf_d[t % 2][0:128, :].rearrange("p (k q) -> p k q", k=KT)
            S.dma("sp", lambda e: e.dma_start(out=gst[2 + i][:], in_=xv[:, kt, :]),
                  writes=[R("gst", 2 + i)], sem=("gst", 2 + i))
            S.dma("sp", lambda e: e.dma_start(out=gst[i][:], in_=gv[:, kt, :]),
                  reads=[R("gbuf", t % 2)], writes=[R("gst", i)], sem=("gst", i))

        def prefetch_g(t):
            if pipe:
                stage_load(t, 0)

        def load_piece(t, kt):
            if not pipe:
                xv = x_d[t].rearrange("p (k q) -> p k q", k=KT)
                S.dma("sp", lambda e: e.dma_start(out=xT[:, kt, :], in_=xv[:, kt, :]),
                      writes=[R("xT", kt)], sem=("xin", kt))
                return
            stage_load(t, kt)
            if kt + 1 < KT:
                stage_load(t, kt + 1)
            i = kt % 2
            STT("dve", xT[:, kt, :], gst[i][:], prm[0][:, 20:21], gst[2 + i][:], ALU.mult, ALU.add,
                [R("gst", i), R("gst", 2 + i), R("prm", 0)], [R("xT", kt)])

        def load_x(t):
            for kt in range(KT):
                load_piece(t, kt)
                norm_sq(kt)

        def send_x(t, outs):
            ob = [R("obuf", c) for c in range(KT)]
            if t + 2 < n_tiles:
                S.dma("pool", lambda e: e.collective_compute("AllGather", ALU.bypass,
                                                             replica_groups=[[0, 1], [2, 3], [4, 5], [6, 7]],
                                                             ins=[obuf_d], outs=[gbuf_d[t % 2]]),
                      reads=ob, writes=[R("gbuf", t % 2)], sem=("cc",), inc=1)
            o = S.dma("sp", lambda e: e.dma_start(out=y_d[t], in_=obuf_d), reads=ob, sem=("yout",))
            outs.append(o)

        def store_y(t, outs):
            o = S.dma("sp", lambda e: e.dma_start(out=y_d[t], in_=xT[:].rearrange("p k q -> p (k q)")),
                      reads=[R("xT", kt) for kt in range(KT)], sem=("yout",))
            outs.append(o)

        def hTr(kt):
            return hT[:, kt, 2:2 + T]

        def mixer(t, l):
            P = prm[l]
            chunks = {}

            def getw(name):
                if name not in chunks:
                    chunks[name] = next_chunk(l, name)
                return chunks[name]

            def g_fm0():
                w, wr = getw("FM0")
                b = bank()
                for kt in range(KT):
                    MM(PS[0:16, b, :], w[:, kt, 0:16], hTr(kt), kt == 0, kt == KT - 1, [wr, R("hT", kt)], [RB(b)])
                CP("act", glrT[0:16, :], PS[0:16, b, :], [RB(b)], [R("glrT")])

            def g_logits():
                for g in range(2):
                    b = bank()
                    for h2 in range(2):
                        nb = 2 * g + h2
                        MM(PS[:, b, h2 * 256:(h2 + 1) * 256], glrT[0:32, nb * 128:(nb + 1) * 128], wa[l][0:32, :],
                           True, True, [R("glrT"), R("wa", l)], [RB(b)])
                    ACT(la[:, 2 * g:2 * g + 2, :], PS[:, b, :].rearrange("p (a q) -> p a q", a=2), AF.Exp,
                        [RB(b)], [R("la", g)], scale=-1.0)
                ACT(la[:], la[:], AF.Ln, [R("la", 0), R("la", 1), R("cvals")], [R("la", 0), R("la", 1)], bias=c_one)

            def g_cumfm():
                for i in range(2):
                    b = bank()
                    for nb in range(NB):
                        MM(PS[:, b, nb * 128:(nb + 1) * 128], la[:, nb, i * 128:(i + 1) * 128], tri_incl,
                           True, True, [R("la", nb // 2), R("cf32")], [RB(b)])
                    ACT(eb[:, i, :], PS[:, b, :], AF.Exp, [RB(b)], [R("eb", i)])
                    ACT(enb[:, i, :], PS[:, b, :], AF.Exp, [RB(b)], [R("enb", i)], scale=-1.0)

            def g_cumtm():
                for g in range(2):
                    b = bank()
                    for h2 in range(2):
                        nb = 2 * g + h2
                        MM(PS[:, b, h2 * 256:(h2 + 1) * 256], tri_excl, la[:, nb, :], True, True,
                           [R("la", g), R("cf32")], [RB(b)])
                    ACT(Eg[:, 2 * g:2 * g + 2, :], PS[:, b, :].rearrange("p (a q) -> p a q", a=2), AF.Exp,
                        [RB(b)], [R("Eg", g)])

            qk_dq = []

            def qknorm(b, raw, rawres, dst, dstres, gcol, is_q, idx):
                q = sqs[idx % 3]
                ACT(q[:], PS[:, b, :], AF.Square, [RB(b)], [R("sqs", idx % 3)])
                CP("dve", raw, PS[:, b, :], [RB(b)], [rawres])
                b2 = bank()
                MM(PS[:, b2, :], ones_bd, q[:], True, True, [R("cbf"), R("sqs", idx % 3)], [RB(b2)])
                r = rs[1 + idx % 2]
                rr = R("rs", 1 + idx % 2)

                def tail():
                    ACT(r[:], PS[:, b2, :], AF.Ln, [RB(b2), R("cvals")], [rr], bias=c_eps)
                    if is_q:
                        ACT(r[:], r[:], AF.Exp, [rr, R("cvals")], [rr], scale=-0.5, bias=c_ln8)
                    else:
                        ACT(r[:], r[:], AF.Exp, [rr], [rr], scale=-0.5)
                    STT("dve", dst, raw, P[:, gcol:gcol + 1], r[:], ALU.mult, ALU.mult,
                        [rawres, R("prm", l), rr], [dstres])
                qk_dq.append(tail)
                while len(qk_dq) > 1:
                    qk_dq.pop(0)()

            def fm_block(name, c0):
                w, wr = getw(name)
                b = bank()
                for kt in range(KT):
                    MM(PS[:, b, :], w[:, kt, c0:c0 + 128], hTr(kt), kt == 0, kt == KT - 1,
                       [wr, R("hT", kt)], [RB(b)])
                return b

            def tm_block(name, nb, ncols):
                w, wr = getw(name)
                b = bank()
                for kt in range(KT):
                    MM(PS[:, b, 0:ncols], hT[:, kt, 2 + nb * 128:2 + (nb + 1) * 128], w[:, kt, :], kt == 0, kt == KT - 1,
                       [wr, R("hT", kt)], [RB(b)])
                return b

            def fm3_blk(i):
                b = fm_block("FM3", i * 128)
                qknorm(b, sqT[:, i, :], R("sqT", i), qn[:, i, :], R("qn", i), 16, True, i)

            def fm4_blk(k):
                b = fm_block("FM4", k * 128)
                qknorm(b, skT[:, k, :], R("skT", k), kn[:, k, :], R("kn", k), 17, False, 4 + k)

            def tm0a_blk(nb):
                b = tm_block("TM0a", nb, 256)
                CP("act", v2[:, nb, :], PS[:, b, 0:256], [RB(b)], [R("v2", nb)])

            def fm1_blk(blk):
                b = fm_block("FM1", blk * 128)
                i = blk % 2
                if blk < 2:
                    STT("dve", qeT[:, i, :], PS[:, b, :], 0.125, eb[:, i, :], ALU.mult, ALU.mult,
                        [RB(b), R("eb", i)], [R("qeT", i)])
                else:
                    TT("dve", keT[:, i, :], PS[:, b, :], enb[:, i, :], ALU.mult, [RB(b), R("enb", i)], [R("keT", i)])

            def fm2_blk(h):
                b = fm_block("FM2", h * 128)
                ACT(sr[:, h, :], PS[:, b, :], AF.Silu, [RB(b)], [R("sr", h)])

            def tm0b_blk(nb):
                b = tm_block("TM0b", nb, 256)
                TT("dve", kd[:, nb, :], PS[:, b, 0:256], Eg[:, nb, :], ALU.mult, [RB(b), R("Eg", nb // 2)], [R("kd", nb)])

            def tm1_blk(nb):
                b = tm_block("TM1", nb, 512)
                CP("act", vtok[:, nb, :], PS[:, b, :], [RB(b)], [R("vtok", nb)])

            S.phase = "mixA_%d" % t
            A = [lambda i=i: fm3_blk(i) for i in range(4)] + [lambda k=k: fm4_blk(k) for k in range(2)] \
                + [lambda nb=nb: tm0a_blk(nb) for nb in range(NB)] \
                + [lambda nb=nb: tm1_blk(nb) for nb in range(NB)]

            def popA(n):
                for _ in range(n):
                    if A:
                        A.pop(0)()

            g_fm0()
            popA(3)
            g_logits()
            popA(3)
            while qk_dq:
                qk_dq.pop(0)()
            g_cumfm()
            popA(4)
            g_cumtm()
            while qk_dq:
                qk_dq.pop(0)()
            popA(len(A))
            Bq = [lambda i=i: fm1_blk(i) for i in range(4)] + [lambda nb=nb: tm0b_blk(nb) for nb in range(NB)]
            Fq = [lambda h=h: fm2_blk(h) for h in range(4)]

            def popB(n):
                for _ in range(n):
                    if Bq:
                        Bq.pop(0)()

            S.phase = "swa_%d" % t
            def swa_stage1(it):
                nb, k = it // 2, it % 2
                jts = [0, 1]
                if t == 0 and nb == 0:
                    jts = [1]
                c0 = jts[0] * 256
                bs = [bank(), bank()]
                srcs = {}
                for jt in jts:
                    if jt == 0:
                        if nb == 0:
                            ksrc, kres = knp[l][:, k, :], R("knp", l)
                            vsrc, vres = v2p[l][:, k * 128:(k + 1) * 128], R("v2p", l)
                        else:
                            ksrc, kres = kn[:, k, (nb - 1) * 128:nb * 128], R("kn", k)
                            vsrc, vres = v2[:, nb - 1, k * 128:(k + 1) * 128], R("v2", nb - 1)
                    else:
                        ksrc, kres = kn[:, k, nb * 128:(nb + 1) * 128], R("kn", k)
                        vsrc, vres = v2[:, nb, k * 128:(k + 1) * 128], R("v2", nb)
                    srcs[jt] = (vsrc, vres)
                    for hh in range(2):
                        fp = hh * 64
                        MM(PS[:, bs[hh], jt * 256:(jt + 1) * 256].rearrange("p (a q) -> p a q", a=2),
                           ksrc[fp:fp + 64, :], qn[fp:fp + 64, 2 * k:2 * k + 2, nb * 128:(nb + 1) * 128],
                           True, True, [kres, R("qn", 2 * k), R("qn", 2 * k + 1)], [RB(bs[hh])])
                pe_, pr = pex[it % 2], R("pex", it % 2)
                pt_, ptr = PT[it % 2], R("PT", it % 2)
                for hh in range(2):
                    ACT(pe_[:, hh, c0:512], PS[:, bs[hh], c0:512], AF.Exp, [RB(bs[hh])], [pr])
                for hh in range(2):
                    TT("pool", pt_[:, hh, c0:512], pe_[:, hh, c0:512], EBt(k, hh)[:, c0:512], ALU.mult, [pr, R("cbf")], [ptr])
                if pipe and t == 2 and nb == 0:
                    S.op("dve", lambda e, pt_=pt_: e.tensor_scalar(out=pt_[:, :, 0:256], in0=pt_[:, :, 0:256],
                                                                   scalar1=P[:, 21:22], scalar2=None, op0=ALU.mult),
                         [ptr, R("prm", l)], [ptr])
                return (nb, k, jts, srcs, pt_, ptr)

            def swa_stage2(st):
                nb, k, jts, srcs, pt_, ptr = st
                bo = bank()
                bd = bank()
                for jt in jts:
                    vsrc, vres = srcs[jt]
                    MM(PS[:, bo, :].rearrange("p (a q) -> p a q", a=2), vsrc, pt_[:, :, jt * 256:(jt + 1) * 256],
                       jt == jts[0], jt == jts[-1], [vres, ptr], [RB(bo)])
                for jt in jts:
                    MM(PS[:, bd, :].rearrange("p (a q) -> p a q", a=2), ones1, pt_[:, :, jt * 256:(jt + 1) * 256],
                       jt == jts[0], jt == jts[-1], [R("cbf"), ptr], [RB(bd)])
                rd, rdr = rden[(2 * nb + k) % 2], R("rden", (2 * nb + k) % 2)
                TT("dve", rd[:], PS[:, bd, :], esink[l][:, k * 512:(k + 1) * 512], ALU.add,
                   [RB(bd), R("esink", l)], [rdr])
                ACT(rd[:], rd[:], AF.Ln, [rdr], [rdr])
                ACT(rd[:], rd[:], AF.Exp, [rdr], [rdr], scale=-1.0)
                for hh in range(2):
                    fp = hh * 64
                    TT("dve", oT[fp:fp + 64, 4 + 2 * k:4 + 2 * k + 2, nb * 128:(nb + 1) * 128],
                       PS[fp:fp + 64, bo, hh * 256:(hh + 1) * 256].rearrange("p (a q) -> p a q", a=2),
                       rd[fp:fp + 64, hh * 256:(hh + 1) * 256].rearrange("p (a q) -> p a q", a=2),
                       ALU.mult, [RB(bo), rdr], [R("oT", 4 + 2 * k), R("oT", 4 + 2 * k + 1)])

            def gla_sb0():
                if "noSb0" not in DEBUG:
                    CP("pool", Sb[:, 0, :, :], Sf[l][:, 0, :, :], [R("Sfq", l, 0, i, hh) for i in range(2) for hh in range(2)], [R("Sb", 0)])

            def gla_at(g):
                bb = [bank(), bank()]
                for nbl in range(2):
                    nb = 2 * g + nbl
                    for i in range(2):
                        for hh in range(2):
                            fp = hh * 64
                            co = (nbl * 2 + i) * 128
                            MM(PS[:, bb[hh], co:co + 128], keT[fp:fp + 64, i, nb * 128:(nb + 1) * 128],
                               qeT[fp:fp + 64, i, nb * 128:(nb + 1) * 128], True, True,
                               [R("keT", i), R("qeT", i)], [RB(bb[hh])])
                for hh in range(2):
                    TT("dve", AT[:, g, hh, :], PS[:, bb[hh], :], maskA, ALU.mult, [RB(bb[hh]), R("cbf")], [R("AT", g, hh)])

            def gla_scan(c):
                nb, hp = c // 2, (c % 2) * 64
                b = bank()
                for i in range(2):
                    MM(PS[:, b, i * 256:(i + 1) * 256], kd[hp:hp + 64, nb, i * 128:(i + 1) * 128],
                       vtok[hp:hp + 64, nb, i * 256:(i + 1) * 256], True, True,
                       [R("kd", nb), R("vtok", nb)], [RB(b)])
                for i in range(2):
                    for hh in range(2):
                        fp = hh * 64
                        col = 64 * c + 63
                        STT("dve", Sf[l][fp:fp + 64, (c + 1) % 2, i, :], Sf[l][fp:fp + 64, c % 2, i, :],
                            eb[fp:fp + 64, i, col:col + 1],
                            PS[fp:fp + 64, b, i * 256 + hh * 128:i * 256 + hh * 128 + 128], ALU.mult, ALU.add,
                            [R("Sfq", l, c % 2, i, hh), R("eb", i), RB(b)], [R("Sfq", l, (c + 1) % 2, i, hh)])
                if c + 1 < NCH:
                    CP("pool", Sb[:, c + 1, :, :], Sf[l][:, (c + 1) % 2, :, :],
                       [R("Sfq", l, (c + 1) % 2, i, hh) for i in range(2) for hh in range(2)], [R("Sb", c + 1)])

            Gq = [gla_sb0] + [lambda g=g: gla_at(g) for g in range(2)] + [lambda c=c: gla_scan(c) for c in range(NCH)]

            def popG(n):
                for _ in range(n):
                    if Gq:
                        Gq.pop(0)()

            pend = None
            for it in range(2 * NB):
                cur = swa_stage1(it)
                if Bq:
                    popB(1)
                else:
                    popG(1)
                if pend is not None:
                    swa_stage2(pend)
                if Bq:
                    popB(1)
                else:
                    popG(2)
                    if Fq:
                        Fq.pop(0)()
                pend = cur
            swa_stage2(pend)
            popB(len(Bq))
            popG(len(Gq))
            while Fq:
                Fq.pop(0)()
            CP("pool", knp[l][:], kn[:, :, T - 128:T], [R("kn", 0), R("kn", 1)], [R("knp", l)])
            CP("pool", v2p[l][:], v2[:, NB - 1, :], [R("v2", NB - 1)], [R("v2p", l)])

            if "stop6" in DEBUG:
                return
            S.phase = "gla_%d" % t
            gl_dq = []
            if "stop4b" in DEBUG:
                return
            for h in range(4):
                i, hh = h // 2, h % 2
                fp = hh * 64
                b = bank()
                for nb in range(NB):
                    aco = ((nb % 2) * 2 + i) * 128
                    MM(PS[:, b, nb * 128:(nb + 1) * 128], vtok[:, nb, h * 128:(h + 1) * 128],
                       AT[:, nb // 2, hh, aco:aco + 128], True, False, [R("vtok", nb), R("AT", nb // 2, hh)], [RB(b)])
                    for half in range(2):
                        c = 2 * nb + half
                        MM(PS[:, b, c * 64:(c + 1) * 64], Sb[fp:fp + 64, c, i, :], qeT[fp:fp + 64, i, c * 64:(c + 1) * 64],
                           False, half == 1, [R("Sb", c), R("qeT", i)], [RB(b)])
                osb = o_sb[h % 2]
                ro = R("o_sb", h % 2)
                q = sqs[h % 3]
                CP("act", osb[:], PS[:, b, :], [RB(b)], [ro])
                ACT(q[:], PS[:, b, :], AF.Square, [RB(b)], [R("sqs", h % 3)])
                r = rs[1 + h % 2]
                rr = R("rs", 1 + h % 2)

                def gtail(r=r, rr=rr, q=q, osb=osb, ro=ro, h=h):
                    b2 = bank()
                    MM(PS[:, b2, :], ones_dv, q[:], True, True, [R("cbf"), R("sqs", h % 3)], [RB(b2)])
                    ACT(r[:], PS[:, b2, :], AF.Ln, [RB(b2), R("cvals")], [rr], bias=c_eps)
                    ACT(r[:], r[:], AF.Exp, [rr], [rr], scale=-0.5)
                    STT("dve", osb[:], osb[:], P[:, 18:19], r[:], ALU.mult, ALU.mult, [ro, R("prm", l), rr], [ro])
                    TT("pool", oT[:, h, :], osb[:], sr[:, h, :], ALU.mult, [ro, R("sr", h)], [R("oT", h)])
                gl_dq.append(gtail)
                while len(gl_dq) > 1:
                    gl_dq.pop(0)()
            while gl_dq:
                gl_dq.pop(0)()

            if "stop5" in DEBUG:
                return
            S.phase = "outproj_%d" % t
            nq = []
            for oc in range(2):
                w, wr = next_chunk(l, "O%d" % oc)
                for cc in range(4):
                    c = 4 * oc + cc
                    b = bank()
                    for kt in range(KT):
                        MM(PS[:, b, :], w[:, kt, cc * 128:(cc + 1) * 128], oT[:, kt, :], kt == 0, kt == KT - 1,
                           [wr, R("oT", kt)], [RB(b)])
                    TT("dve", xT[:, c, :], PS[:, b, :], xT[:, c, :], ALU.add, [RB(b), R("xT", c)], [R("xT", c)])
                    nq.append(c)
                    while len(nq) > 2:
                        norm_sq(nq.pop(0))
            while nq:
                norm_sq(nq.pop(0))

        def ffn(t, l):
            P = prm[l]
            S.phase = "ffnnorm_%d" % t
            CP("pool", hT[:, :, 0:2], h2halo[l][:], [R("h2halo", l)], [R("hThalo")])
            rmsnorm(l, 8)
            S.phase = "ffnup_%d" % t
            CP("pool", h2halo[l][:], hT[:, :, T:T + 2], [R("hT", kt) for kt in range(KT)], [R("h2halo", l)])
            dq = []

            def flush(keep):
                while len(dq) > keep:
                    dq.pop(0)()

            for ci in range(11):
                w, wr = next_chunk(l, "U%d" % ci)
                for blk in range(4):
                    isb = blk >= 2
                    j = 2 * ci + blk % 2
                    fidx = j + (FT if isb else 0)
                    pc = 32 + fidx * 4
                    p = bank_pair()
                    rb = [RB(2 * p), RB(2 * p + 1)]
                    for kt in range(KT):
                        for half in range(2):
                            MM(PS[:, 2 * p + half, 0:258], w[:, kt, blk * 128:(blk + 1) * 128],
                               hT[:, kt, half * 256:half * 256 + 258], kt == 0, kt == KT - 1,
                               [wr, R("hT", kt), R("hThalo")], [rb[half]])
                    fbi = ci * 4 + blk
                    tt, ttr = t1[fbi % 3], R("t1", fbi % 3)
                    uidx = ((ci % 2) * 2 + blk) if not isb else (4 + blk - 2)
                    uu, uur = ub[uidx], R("u", uidx)
                    ACT(tt[:], PS[:, 2 * p:2 * p + 2, 2:258], AF.Identity, rb + [R("prm", l)], [ttr],
                        scale=P[:, pc + 2:pc + 3], bias=P[:, pc + 3:pc + 4])
                    STT("dve", tt[:], PS[:, 2 * p:2 * p + 2, 1:257], P[:, pc + 1:pc + 2], tt[:], ALU.mult, ALU.add,
                        rb + [R("prm", l), ttr], [ttr])
                    STT("dve", uu[:], PS[:, 2 * p:2 * p + 2, 0:256], P[:, pc:pc + 1], tt[:], ALU.mult, ALU.add,
                        rb + [R("prm", l), ttr], [uur])
                    if not isb:
                        dq.append(lambda uu=uu, uur=uur: ACT(uu[:], uu[:], AF.Silu, [uur], [uur]))
                    else:
                        aidx = (ci % 2) * 2 + blk - 2
                        ua, uar = ub[aidx], R("u", aidx)
                        dq.append(lambda ua=ua, uar=uar, uu=uu, uur=uur, j=j: TT(
                            "pool", gT[:, j, :].rearrange("p (a q) -> p a q", a=2), ua[:], uu[:], ALU.mult,
                            [uar, uur], [R("gT", j)]))
                    flush(2)
            flush(0)
            S.phase = "ffndown_%d" % t
            prefetch_g(t + 1)
            lq = []
            for c in range(8):
                w, wr = next_chunk(l, "D%d" % c)
                b = bank()
                for ft in range(FT):
                    MM(PS[:, b, :], w[:, ft, :], gT[:, ft, :], ft == 0, ft == FT - 1, [wr, R("gT", ft)], [RB(b)])
                TT("dve", xT[:, c, :], PS[:, b, :], xT[:, c, :], ALU.add, [RB(b), R("xT", c)], [R("xT", c)])
                if pipe:
                    S.dma("sp", lambda e, c=c: e.dma_start(out=obuf_d[:, c * T:(c + 1) * T], in_=xT[:, c, :]),
                          reads=[R("xT", c)], writes=[R("obuf", c)], sem=("obuf", c))
                    if t + 1 < n_tiles:
                        load_piece(t + 1, c)
                        lq.append(c)
                        while len(lq) > 2:
                            norm_sq(lq.pop(0))
            while lq:
                norm_sq(lq.pop(0))

        outs = []
        if pipe:
            for kt in range(KT):
                MEMSET("dve", xT[:, kt, :], 0.0, [R("xT", kt)])
            for i in range(2):
                S.dma("sp", lambda e, i=i: e.dma_start(out=gbuf_d[i][0:128, :], in_=xT[:].rearrange("p k q -> p (k q)")),
                      reads=[R("xT", kt) for kt in range(KT)], writes=[R("gbuf", i)], sem=("ginit", i))
        for t in range(n_tiles):
            S.phase = "loadx_%d" % t
            if t == 0 or not pipe:
                load_x(t)
            for l in range(NL):
                if "nolayer" in DEBUG:
                    continue
                S.phase = "mixnorm_%d" % t
                rmsnorm(l, 0)
                if "nomixer" not in DEBUG:
                    mixer(t, l)
                if "noffn" not in DEBUG:
                    ffn(t, l)
            S.phase = "send_%d" % t
            if pipe:
                send_x(t, outs)
            else:
                store_y(t, outs)
        S.annotate = "annotate" in DEBUG
        S.emit(final_ops=outs)
    return nc


def alibi_slopes(n):
    return np.array([2.0 ** (-8.0 * (h + 1) / n) for h in range(n)], dtype=np.float64)


def make_consts():
    c = np.zeros((128, 384 + 3072), np.float32)
    c[:, 0:128] = np.eye(128, dtype=np.float32)
    s = np.arange(128)[:, None]
    q = np.arange(128)[None, :]
    same = (s // 64) == (q // 64)
    c[:, 128:256] = np.where(same & (s <= q), -1.0 / 16.0, 0.0)
    c[:, 256:384] = np.where(same & (s > q), -1.0 / 16.0, 0.0)
    o = 384
    c[:, o + 0:o + 128] = 1.0 / 1024.0
    bd = (np.arange(128)[:, None] // 64) == (np.arange(128)[None, :] // 64)
    c[:, o + 128:o + 256] = np.where(bd, 1.0 / 64.0, 0.0)
    c[:, o + 256:o + 384] = 1.0 / 128.0
    c[:, o + 384:o + 512] = 1.0
    mA = np.where(same & (s <= q), 1.0, 0.0)
    c[:, o + 512:o + 1024] = np.tile(mA, (1, 4))
    sl = alibi_slopes(8)
    for k in range(2):
        for hh in range(2):
            tab = np.zeros((128, 512), np.float64)
            for jt in range(2):
                for ii in range(2):
                    h = 4 * k + 2 * ii + hh
                    if jt == 1:
                        dist = (q - s).astype(np.float64)
                        keep = q >= s
                    else:
                        dist = (128 + q - s).astype(np.float64)
                        keep = q < s
                    co = jt * 256 + ii * 128
                    tab[:, co:co + 128] = np.where(keep, np.exp(-sl[h] * dist), 0.0)
            oo = o + 1024 + (2 * k + hh) * 512
            c[:, oo:oo + 512] = tab
    return c


def pack_params(inp, l):
    p = np.zeros((128, NPRM), np.float32)
    p[:, 0:8] = inp["mix_norm"][l].reshape(8, 128).T
    p[:, 8:16] = inp["ffn_norm"][l].reshape(8, 128).T
    p[:, 16] = np.tile(inp["q_norm"][l], 2)
    p[:, 17] = np.tile(inp["k_norm"][l], 2)
    p[:, 18] = inp["gla_norm"][l]
    cw = inp["conv_w"][l].reshape(3, 44, 128)
    cb = inp["conv_b"][l].reshape(44, 128)
    blk = np.zeros((128, 44, 4), np.float32)
    blk[:, :, 0:3] = cw.transpose(2, 1, 0)
    blk[:, :, 3] = cb.T
    p[:, 32:32 + 176] = blk.reshape(128, 176)
    return p


def pack_sinks(inp, l):
    s = np.zeros((128, 1024), np.float32)
    for k in range(2):
        for hh in range(2):
            for ii in range(2):
                h = 4 * k + 2 * ii + hh
                slot = hh * 2 + ii
                s[:, k * 512 + slot * 128:k * 512 + (slot + 1) * 128] = inp["sinks"][l][h]
    return s


def pack_wa(inp, l):
    w = np.zeros((32, 256), np.float32)
    w[0:16] = inp["w_alpha2"][l]
    w[16] = inp["b_alpha"][l]
    return w


def to_fm(xs):
    n = xs.shape[0] // T
    return np.ascontiguousarray(xs.reshape(n, T, KT, 128).transpose(0, 3, 2, 1).reshape(n, 128, KT * T))


def from_fm(yf):
    n = yf.shape[0]
    return np.ascontiguousarray(yf.reshape(n, 128, KT, T).transpose(0, 3, 2, 1).reshape(n * T, D))


_NC_CACHE = {}


def get_nc(n_tiles, n_layers, pipe=False):
    key = (n_tiles, n_layers, pipe)
    if key not in _NC_CACHE:
        _NC_CACHE[key] = build(n_tiles, n_layers, pipe)
    return _NC_CACHE[key]


def host_maps(inp, seqs, n_layers, layer0=0, roles=None):
    ls = list(range(layer0, layer0 + n_layers))
    f = lambda a: np.ascontiguousarray(np.asarray(a, dtype=np.float32))

    def shared_for(ls):
        return {
            "w_in": f(np.asarray(inp["w_in"])[ls]),
            "w_out": f(np.asarray(inp["w_out"])[ls]),
            "w_up": f(np.asarray(inp["w_up"])[ls]),
            "w_down": f(np.asarray(inp["w_down"])[ls]),
            "prm": np.stack([pack_params(inp, l) for l in ls]),
            "snk": np.stack([pack_sinks(inp, l) for l in ls]),
            "wa": np.stack([pack_wa(inp, l) for l in ls]),
            "cst": make_consts(),
        }

    maps = []
    if roles is None:
        shared = shared_for(ls)
        for s in seqs:
            m = dict(shared)
            m["x"] = f(s)
            maps.append(m)
        return maps
    per_layer = {}
    for s, role in zip(seqs, roles):
        if role not in per_layer:
            per_layer[role] = shared_for([role])
        m = dict(per_layer[role])
        prm = m["prm"].copy()
        prm[:, :, 20] = float(role)
        prm[:, :, 21] = 1.0 - float(role)
        m["prm"] = prm
        m["x"] = f(s)
        maps.append(m)
    return maps


def kernel(**inputs):
    inp = {k: np.asarray(v) for k, v in inputs.items()}
    x = inp["x"]
    B, L, _ = x.shape
    n_tiles = L // T
    nc = get_nc(n_tiles + 2, 1, True)
    seqs, roles = [], []
    zt = np.zeros((2 * T, D), np.float32)
    for c in range(8):
        b, role = c // 2, c % 2
        if role == 0:
            seqs.append(to_fm(np.concatenate([x[b], zt], axis=0)))
        else:
            seqs.append(np.zeros((n_tiles + 2, 128, KT * T), np.float32))
        roles.append(role)
    maps = host_maps(inp, seqs, 1, roles=roles)
    res = run_bass_kernel_spmd(nc, maps, core_ids=list(range(8)))
    out = np.stack([from_fm(np.asarray(res.results[2 * b + 1]["y"]))[2 * T:] for b in range(B)], axis=0)
    return out.astype(np.float32)
```
